# Optimizing a Trainium2 kernel written in Bass

```python
import jax, jax.numpy as jnp
from jax import lax
import numpy as np

D_MODEL = 1024
BATCH = 1
SEQ = 16384
DEPTH = 4

HEAD_DIM = 64
MIX_WIDTH = D_MODEL
ATTN_WIDTH = MIX_WIDTH // 2
ATTN_Q_HEADS = ATTN_WIDTH // HEAD_DIM
ATTN_KV_HEADS = 2
KV_WIDTH = ATTN_KV_HEADS * HEAD_DIM
WINDOW = 128
ATTN_BLOCK = 128
RWKV_WIDTH = MIX_WIDTH // 4
RWKV_HEADS = RWKV_WIDTH // HEAD_DIM
DECAY_LORA = 64
ICLR_LORA = 64
RWKV_LN_EPS = 64e-5
GLA_VAL_WIDTH = MIX_WIDTH // 4
GLA_HEADS = 4
GLA_VAL_DIM = GLA_VAL_WIDTH // GLA_HEADS
GLA_KEY_WIDTH = GLA_VAL_WIDTH // 2
GLA_KEY_DIM = GLA_KEY_WIDTH // GLA_HEADS
GLA_GATE_LORA = 16
GLA_GATE_NORMALIZER = 16.0
GLA_CHUNK = 64
IN_SPLITS = (ATTN_WIDTH, KV_WIDTH, KV_WIDTH, ATTN_WIDTH,
             3 * RWKV_WIDTH, RWKV_WIDTH,
             GLA_KEY_WIDTH, GLA_KEY_WIDTH, GLA_VAL_WIDTH, GLA_VAL_WIDTH)
IN_WIDTH = 2 * ATTN_WIDTH + 2 * KV_WIDTH + 4 * RWKV_WIDTH + 2 * GLA_KEY_WIDTH + 2 * GLA_VAL_WIDTH
NORM_EPS = 1e-6

kernel_name = "hybrid_swa_rwkv7_gla_parallel_heads"


def _rmsnorm(x, g):
    xf = x.astype(jnp.float32)
    y = xf * lax.rsqrt(jnp.mean(xf * xf, axis=-1, keepdims=True) + NORM_EPS)
    return (y * g.astype(jnp.float32)).astype(x.dtype)


def _shift(z):
    return jnp.pad(z, ((0, 0), (1, 0), (0, 0)))[:, :-1]


def _sliding_window_attention(q, k, v, sinks):
    B, T, Hq, D = q.shape
    Hkv = k.shape[2]
    G = Hq // Hkv
    nb = T // ATTN_BLOCK
    qb = q.reshape(B, nb, ATTN_BLOCK, Hkv, G, D)

    def band(z):
        zb = z.reshape(B, nb, ATTN_BLOCK, Hkv, D)
        prev = jnp.concatenate([jnp.zeros_like(zb[:, :1]), zb[:, :-1]], axis=1)
        return jnp.concatenate([prev, zb], axis=2)

    kb, vb = band(k), band(v)
    s = jnp.einsum('bnqhgd,bnkhd->bnhgqk', qb, kb,
                   preferred_element_type=jnp.float32) * (D ** -0.5)
    qi = jnp.arange(ATTN_BLOCK)[:, None]
    kj = jnp.arange(2 * ATTN_BLOCK)[None, :]
    dist = qi - kj + ATTN_BLOCK
    key_pos = jnp.arange(nb)[:, None, None] * ATTN_BLOCK + kj[None] - ATTN_BLOCK
    valid = (dist >= 0) & (dist < WINDOW) & (key_pos >= 0)
    slopes = 2.0 ** (-8.0 * jnp.arange(1, Hq + 1, dtype=jnp.float32) / Hq)
    s = s - slopes.reshape(Hkv, G)[:, :, None, None] * dist.astype(jnp.float32)
    s = jnp.where(valid[None, :, None, None], s, -jnp.inf)
    sink = jnp.broadcast_to(sinks.astype(jnp.float32).reshape(Hkv, G)[:, :, None, None],
                            s.shape[:-1] + (1,))
    p = jax.nn.softmax(jnp.concatenate([s, sink], axis=-1), axis=-1)[..., :-1]
    o = jnp.einsum('bnhgqk,bnkhd->bnqhgd', p.astype(v.dtype), vb)
    return o.reshape(B, T, Hq * D)


def _rwkv7_mixer(h, r, k, v, mu_w, mu_a, w0, w1, w2, a0, a1, a2, k_k, k_a, r_k, ln_w, ln_b):
    B, T, _ = h.shape
    f32 = jnp.float32
    h_prev = _shift(h)
    xw = h + (h_prev - h) * mu_w
    xa = h + (h_prev - h) * mu_a
    w = -jax.nn.softplus(-(w0 + jnp.tanh(xw @ w1) @ w2).astype(f32)) - 0.5
    decay = jnp.exp(-jnp.exp(w))
    a = jax.nn.sigmoid((a0 + (xa @ a1) @ a2).astype(f32))
    r, k, v = r.astype(f32), k.astype(f32), v.astype(f32)
    heads = lambda z: z.reshape(B, T, RWKV_HEADS, HEAD_DIM)
    kk = heads(k * k_k.astype(f32))
    kk = kk / jnp.maximum(jnp.linalg.norm(kk, axis=-1, keepdims=True), 1e-12)
    k = k * (1.0 + (a - 1.0) * k_a.astype(f32))

    def step(S, inp):
        r_t, w_t, k_t, v_t, kk_t, a_t = inp
        sa = jnp.einsum('bhvk,bhk->bhv', S, -kk_t)
        S = (S * w_t[:, :, None, :] + sa[..., None] * (kk_t * a_t)[:, :, None, :]
             + v_t[..., None] * k_t[:, :, None, :])
        return S, jnp.einsum('bhvk,bhk->bhv', S, r_t)

    tm = lambda z: jnp.moveaxis(z, 1, 0)
    S0 = jnp.zeros((B, RWKV_HEADS, HEAD_DIM, HEAD_DIM), f32)
    _, y = lax.scan(step, S0, (tm(heads(r)), tm(heads(decay)), tm(heads(k)), tm(heads(v)),
                               tm(kk), tm(heads(a))))
    y = jnp.moveaxis(y, 0, 1)
    mean = jnp.mean(y, axis=-1, keepdims=True)
    var = jnp.mean(jnp.square(y - mean), axis=-1, keepdims=True)
    y = ((y - mean) * lax.rsqrt(var + RWKV_LN_EPS)).reshape(B, T, RWKV_WIDTH)
    y = y * ln_w.astype(f32) + ln_b.astype(f32)
    bonus = jnp.sum(heads(r) * heads(k) * r_k.astype(f32), axis=-1, keepdims=True) * heads(v)
    return (y + bonus.reshape(B, T, RWKV_WIDTH)).astype(h.dtype)


def _gla_mixer(h, q, k, v, gk1, gk2, gk_b, norm_w):
    B, T, _ = h.shape
    f32 = jnp.float32
    n = T // GLA_CHUNK
    gk = jax.nn.log_sigmoid(((h @ gk1) @ gk2 + gk_b).astype(f32)) / GLA_GATE_NORMALIZER

    def chunks(z, d):
        return z.reshape(B, n, GLA_CHUNK, GLA_HEADS, d).transpose(1, 0, 3, 2, 4).astype(f32)

    qc = chunks(q, GLA_KEY_DIM) * (GLA_KEY_DIM ** -0.5)
    kc = chunks(k, GLA_KEY_DIM)
    vc = chunks(v, GLA_VAL_DIM)
    gc = chunks(gk, GLA_KEY_DIM)
    causal = jnp.tril(jnp.ones((GLA_CHUNK, GLA_CHUNK), bool))

    def step(S, inp):
        q_c, k_c, v_c, g_c = inp
        b = jnp.cumsum(g_c, axis=2)
        diff = b[:, :, :, None, :] - b[:, :, None, :, :]
        diff = jnp.where(causal[:, :, None], diff, -jnp.inf)
        A = jnp.einsum('bhid,bhjd,bhijd->bhij', q_c, k_c, jnp.exp(diff))
        o = A @ v_c + jnp.einsum('bhid,bhdv->bhiv', q_c * jnp.exp(b), S)
        b_last = b[:, :, -1:, :]
        S = (jnp.exp(b_last[:, :, 0, :])[..., None] * S
             + jnp.einsum('bhjd,bhjv->bhdv', k_c * jnp.exp(b_last - b), v_c))
        return S, o

    S0 = jnp.zeros((B, GLA_HEADS, GLA_KEY_DIM, GLA_VAL_DIM), f32)
    _, o = lax.scan(step, S0, (qc, kc, vc, gc))
    o = o.transpose(1, 0, 3, 2, 4).reshape(B, T, GLA_HEADS, GLA_VAL_DIM)
    o = o * lax.rsqrt(jnp.mean(o * o, axis=-1, keepdims=True) + 1e-5) * norm_w.astype(f32)
    return o.reshape(B, T, GLA_VAL_WIDTH).astype(h.dtype)


def _hybrid_layer(x, c_act, ada_w, ada_b, g_pre, g_post, w_in, w_out, sinks,
                  mu_rkv, mu_w, mu_a, w0, w1, w2, a0, a1, a2, k_k, k_a, r_k, ln_w, ln_b,
                  gk1, gk2, gk_b, gla_norm_w):
    B, T, _ = x.shape
    shift, scale, gate = jnp.split(c_act @ ada_w + ada_b, 3, axis=-1)
    h = _rmsnorm(x, g_pre) * (1.0 + scale[:, None]) + shift[:, None]
    proj = h @ w_in
    aq, ak, av, ag, rkv, rg, gq, gkk, gv, gg = jnp.split(proj, list(np.cumsum(IN_SPLITS)[:-1]), axis=-1)
    attn = _sliding_window_attention(aq.reshape(B, T, ATTN_Q_HEADS, HEAD_DIM),
                                     ak.reshape(B, T, ATTN_KV_HEADS, HEAD_DIM),
                                     av.reshape(B, T, ATTN_KV_HEADS, HEAD_DIM), sinks)
    attn = attn * jax.nn.silu(ag)
    rkv = rkv + (_shift(rkv) - rkv) * mu_rkv
    rr, rk, rv = jnp.split(rkv, 3, axis=-1)
    rwkv = _rwkv7_mixer(h, rr, rk, rv, mu_w, mu_a, w0, w1, w2, a0, a1, a2, k_k, k_a, r_k, ln_w, ln_b)
    rwkv = rwkv * jax.nn.silu(rg)
    gla = _gla_mixer(h, gq, gkk, gv, gk1, gk2, gk_b, gla_norm_w) * jax.nn.silu(gg)
    y = jnp.concatenate([attn, rwkv, gla], axis=-1) @ w_out
    return x + gate[:, None] * _rmsnorm(y, g_post)


def setup_inputs(seed: int = 0) -> dict:
    key = jax.random.key(seed)
    ks = jax.random.split(key, 32)
    nrm = lambda k, shape, s: s * jax.random.normal(k, shape, jnp.float32)
    uni = lambda k, shape, lo, hi: jax.random.uniform(k, shape, jnp.float32, lo, hi)
    D, L = D_MODEL, DEPTH
    return {
        "x": nrm(ks[0], (BATCH, SEQ, D), 1.0),
        "c": nrm(ks[1], (BATCH, D), 1.0),
        "ada_w": nrm(ks[2], (L, D, 3 * D), 0.5 * D ** -0.5),
        "ada_b": nrm(ks[3], (L, 3 * D), 0.02),
        "norm_pre": 1.0 + nrm(ks[4], (L, D), 0.05),
        "norm_post": 1.0 + nrm(ks[5], (L, D), 0.05),
        "w_in": nrm(ks[6], (L, D, IN_WIDTH), D ** -0.5),
        "w_out": nrm(ks[7], (L, MIX_WIDTH, D), MIX_WIDTH ** -0.5),
        "attn_sinks": nrm(ks[8], (L, ATTN_Q_HEADS), 0.5),
        "rwkv_mu_rkv": uni(ks[9], (L, 3 * RWKV_WIDTH), 0.0, 1.0),
        "rwkv_mu_w": uni(ks[10], (L, D), 0.0, 1.0),
        "rwkv_mu_a": uni(ks[11], (L, D), 0.0, 1.0),
        "rwkv_w0": uni(ks[12], (L, RWKV_WIDTH), -4.0, 1.0),
        "rwkv_w1": nrm(ks[13], (L, D, DECAY_LORA), D ** -0.5),
        "rwkv_w2": nrm(ks[14], (L, DECAY_LORA, RWKV_WIDTH), 0.5 * DECAY_LORA ** -0.5),
        "rwkv_a0": nrm(ks[15], (L, RWKV_WIDTH), 0.1),
        "rwkv_a1": nrm(ks[16], (L, D, ICLR_LORA), D ** -0.5),
        "rwkv_a2": nrm(ks[17], (L, ICLR_LORA, RWKV_WIDTH), 0.5 * ICLR_LORA ** -0.5),
        "rwkv_k_k": 0.85 + nrm(ks[18], (L, RWKV_WIDTH), 0.05),
        "rwkv_k_a": 1.0 + nrm(ks[19], (L, RWKV_WIDTH), 0.05),
        "rwkv_r_k": nrm(ks[20], (L, RWKV_HEADS, HEAD_DIM), 0.1),
        "rwkv_ln_w": 1.0 + nrm(ks[21], (L, RWKV_WIDTH), 0.05),
        "rwkv_ln_b": nrm(ks[22], (L, RWKV_WIDTH), 0.02),
        "gla_gk1": nrm(ks[23], (L, D, GLA_GATE_LORA), D ** -0.5),
        "gla_gk2": nrm(ks[24], (L, GLA_GATE_LORA, GLA_KEY_WIDTH), GLA_GATE_LORA ** -0.5),
        "gla_gk_b": nrm(ks[25], (L, GLA_KEY_WIDTH), 0.1),
        "gla_norm_w": 1.0 + nrm(ks[26], (L, GLA_VAL_DIM), 0.05),
    }


def reference(x, c, ada_w, ada_b, norm_pre, norm_post, w_in, w_out, attn_sinks,
              rwkv_mu_rkv, rwkv_mu_w, rwkv_mu_a, rwkv_w0, rwkv_w1, rwkv_w2,
              rwkv_a0, rwkv_a1, rwkv_a2, rwkv_k_k, rwkv_k_a, rwkv_r_k, rwkv_ln_w, rwkv_ln_b,
              gla_gk1, gla_gk2, gla_gk_b, gla_norm_w):
    c_act = jax.nn.silu(c)
    for l in range(DEPTH):
        x = _hybrid_layer(x, c_act, ada_w[l], ada_b[l], norm_pre[l], norm_post[l], w_in[l], w_out[l],
                          attn_sinks[l], rwkv_mu_rkv[l], rwkv_mu_w[l], rwkv_mu_a[l],
                          rwkv_w0[l], rwkv_w1[l], rwkv_w2[l], rwkv_a0[l], rwkv_a1[l], rwkv_a2[l],
                          rwkv_k_k[l], rwkv_k_a[l], rwkv_r_k[l], rwkv_ln_w[l], rwkv_ln_b[l],
                          gla_gk1[l], gla_gk2[l], gla_gk_b[l], gla_norm_w[l])
    return x
```

```python
import numpy as np
from contextlib import ExitStack
import concourse.bass as bass
import concourse.mybir as mybir
from concourse.bass_utils import run_bass_kernel_spmd

F32 = mybir.dt.float32
BF16 = mybir.dt.bfloat16
AF = mybir.ActivationFunctionType
ALU = mybir.AluOpType
AX = mybir.AxisListType

ENGINES = ("pe", "act", "dve", "pool", "sp")
D = 1024
INW = 3072
NCOL = 89
C_GPRE, C_GPOST, C_MUW, C_MUA, C_ADAB = 0, 8, 16, 24, 32
C_W0, C_A0, C_KK, C_KA, C_RK, C_LNW, C_LNB, C_GKB, C_GNW, C_SINK, C_C = 56, 58, 60, 62, 64, 66, 68, 70, 71, 73, 81
K_ID, K_MU2, K_MLS, K_BD64, K_BDM, K_MI4, K_HM, K_AB, K_AB0, K_ONES, K_END = 0, 128, 384, 512, 640, 896, 1408, 1412, 1668, 1924, 2052
SLOPES = [float(2.0 ** (-8.0 * (h + 1) / 8)) for h in range(8)]
DECAY_K = float(np.exp(-0.5))


class Buf:
    __slots__ = ("name", "w", "r")

    def __init__(self, name):
        self.name = name
        self.w = None
        self.r = []


class Op:
    __slots__ = ("eng", "idx", "fn", "deps", "dma_key", "dma_cnt", "inc")

    def __init__(self, eng, idx, fn):
        self.eng = eng
        self.idx = idx
        self.fn = fn
        self.deps = []
        self.dma_key = None
        self.dma_cnt = 0
        self.inc = 16


class Sched:
    def __init__(self, nc):
        self.nc = nc
        self.ops = {e: [] for e in ENGINES}
        self.bufs = {}
        self.dma_tot = {}

    def buf(self, key):
        b = self.bufs.get(key)
        if b is None:
            b = Buf(key)
            self.bufs[key] = b
        return b

    def _bl(self, xs):
        return [x if isinstance(x, Buf) else self.buf(x) for x in xs]

    def add(self, eng, fn, reads=(), writes=(), dma_key=None, inc=16):
        reads = self._bl(reads)
        writes = self._bl(writes)
        op = Op(eng, len(self.ops[eng]), fn)
        deps = set()
        for b in reads:
            if b.w is not None:
                deps.add(b.w)
        for b in writes:
            if b.w is not None:
                deps.add(b.w)
            for r in b.r:
                deps.add(r)
        op.deps = list(deps)
        for b in reads:
            b.r.append(op)
        for b in writes:
            b.w = op
            b.r = []
        if dma_key is not None:
            op.dma_key = dma_key
            op.inc = inc
            self.dma_tot[dma_key] = self.dma_tot.get(dma_key, 0) + inc
            op.dma_cnt = self.dma_tot[dma_key]
        self.ops[eng].append(op)
        return op

    def nop(self, eng, deps):
        op = Op(eng, len(self.ops[eng]), None)
        op.deps = list(deps)
        self.ops[eng].append(op)
        return op

    def barrier(self):
        last = [self.ops[e][-1] for e in ENGINES if self.ops[e]]
        lastdma = {}
        for e in ENGINES:
            for o in self.ops[e]:
                if o.dma_key is not None:
                    lastdma[o.dma_key] = o
        deps = last + list(lastdma.values())
        for e in ENGINES:
            self.nop(e, deps)

    def emit(self, block, sems, dsems):
        ops = self.ops

        def run(ename, eng):
            waited = {}
            for op in ops[ename]:
                need = {}
                for d in op.deps:
                    if d.dma_key is not None:
                        k = ("d", d.dma_key)
                        v = d.dma_cnt
                    else:
                        if d.eng == ename and (ename == "pe" or d.idx >= op.idx):
                            continue
                        k = ("e", d.eng)
                        v = d.idx + 1
                    if v > need.get(k, 0):
                        need[k] = v
                for k, v in need.items():
                    if waited.get(k, 0) >= v:
                        continue
                    waited[k] = v
                    s = dsems[k[1]] if k[0] == "d" else sems[k[1]]
                    eng.wait_ge(s, v)
                if op.fn is None:
                    eng.sem_inc(sems[ename], 1)
                    continue
                ins = op.fn(eng)
                if op.dma_key is not None:
                    ins.then_inc(dsems[op.dma_key], op.inc)
                    eng.sem_inc(sems[ename], 1)
                else:
                    ins.then_inc(sems[ename], 1)

        @block.tensor
        def _(e):
            run("pe", e)

        @block.scalar
        def _(e):
            run("act", e)

        @block.vector
        def _(e):
            run("dve", e)

        @block.gpsimd
        def _(e):
            run("pool", e)

        @block.sync
        def _(e):
            run("sp", e)


def build_program(T, L, debug=None):
    NT = T // 128
    nc = bass.Bass("TRN2", target_bir_lowering=False)
    dI = lambda name, shape: nc.dram_tensor(name, list(shape), F32, kind="ExternalInput").ap()
    x_d = dI("x", [T, D])
    win_d = dI("w_in", [L, D, INW])
    wout_d = dI("w_out", [L, D, D])
    adaw_d = dI("ada_w", [L, D, 3 * D])
    w1_d = dI("rwkv_w1", [L, D, 64])
    a1_d = dI("rwkv_a1", [L, D, 64])
    w2_d = dI("rwkv_w2", [L, 64, 256])
    a2_d = dI("rwkv_a2", [L, 64, 256])
    gk1_d = dI("gla_gk1", [L, D, 16])
    gk2_d = dI("gla_gk2", [L, 16, 128])
    murkv_d = dI("rwkv_mu_rkv", [L, 768])
    pcol_d = dI("pcol", [L, 128, NCOL])
    cst_d = dI("cst", [128, K_END])
    y_d = nc.dram_tensor("y", [T, D], F32, kind="ExternalOutput").ap()
    xs_d = [nc.dram_tensor("xscr%d" % i, [T, D], F32).ap() for i in range(2)]
    dbg_out = {}

    S = Sched(nc)
    es = ExitStack()
    with es:
        def sb(name, shape, dt=F32):
            return es.enter_context(nc.sbuf_tensor(name, list(shape), dt))

        banks = [es.enter_context(nc.psum_tensor("pb%d" % i, [128, 512], F32)) for i in range(8)]
        bank_ctr = [0]

        def nb():
            i = bank_ctr[0] % 8
            bank_ctr[0] += 1
            return banks[i], "pb%d" % i

        CST = sb("CST", [128, K_END])
        IDb = sb("IDb", [128, 128], BF16)
        BD64S = sb("BD64S", [128, 128])
        PCOL = sb("PCOL", [128, NCOL])
        PC2 = sb("PC2", [128, 48])
        CACT = sb("CACT", [128, 8])
        WIN = sb("WIN", [128, 8, INW], BF16)
        WRB = sb("WRB", [128, 8, 768], BF16)
        WKD = sb("WKD", [128, 8, 256], BF16)
        WOUT = sb("WOUT", [128, 8, D], BF16)
        WL1A = sb("WL1A", [128, 8, 128], BF16)
        WL1B = sb("WL1B", [128, 8, 128], BF16)
        WG1 = sb("WG1", [128, 8, 16], BF16)
        W2P = sb("W2P", [128, 256], BF16)
        A2P = sb("A2P", [128, 256], BF16)
        GK2 = sb("GK2", [16, 128], BF16)
        STG = [sb("STG%d" % i, [128, 1536]) for i in range(2)]
        LST = sb("LST", [128, 8, 144])
        L2ST = sb("L2ST", [128, 256])
        G2ST = sb("G2ST", [16, 128])
        GG = sb("GG", [128, D])
        DG = sb("DG", [128, 128])
        XT = [sb("XT%d" % i, [128, D]) for i in range(2)]
        XO = [sb("XO%d" % i, [128, D]) for i in range(2)]
        XN = sb("XN", [128, D])
        MUR = XN
        OMMUR = XO[0]
        SC = sb("SC", [128, 16])
        HT = [sb("HT%d" % i, [128, 8, 130], BF16) for i in range(2)]
        QT = sb("QT", [128, 512], BF16)
        KT = [sb("KT%d" % i, [128, 256], BF16) for i in range(2)]
        AV = [sb("AV%d" % i, [128, 128], BF16) for i in range(2)]
        SAG = sb("SAG", [128, 512], BF16)
        SRGG = sb("SRGG", [128, 512], BF16)
        RKV = sb("RKV", [128, 6, 128])
        GQK = sb("GQK", [128, 256])
        L1b = sb("L1b", [128, 128], BF16)
        G1b = sb("G1b", [16, 128], BF16)
        RVb = sb("RVb", [128, 256], BF16)
        GVb = sb("GVb", [128, 256], BF16)
        CAT = sb("CAT", [128, 8, 128], BF16)
        SS = [sb("SS%d" % i, [128, 256]) for i in range(2)]
        PP = [sb("PP%d" % i, [128, 256], BF16) for i in range(2)]
        DGb = [sb("DGb%d" % i, [128, 128], BF16) for i in range(2)]
        PTN = [sb("PTN%d" % i, [128, 256], BF16) for i in range(2)]
        ASC = [sb("ASC%d" % i, [128, 8]) for i in range(2)]
        ELL = sb("ELL", [128, 2, 128]); CS = sb("CS", [128, 2, 128]); AA = sb("AA", [128, 2, 128])
        EP = sb("EP", [128, 2, 128]); EI = sb("EI", [128, 2, 128]); EPV = sb("EPV", [128, 2, 128]); EE = sb("EE", [128, 2, 128])
        KKt = sb("KKt", [128, 2, 128]); KK2 = sb("KK2", [128, 2, 128]); KKN = sb("KKN", [128, 2, 128]); TKA = sb("TKA", [128, 2, 128])
        KP = sb("KP", [128, 2, 128]); TMP = sb("TMP", [128, 2, 128]); PRD = sb("PRD", [128, 2, 128]); BON = sb("BON", [128, 2, 128])
        AR = sb("AR", [128, 2, 256], BF16)
        BTIL = sb("BTIL", [128, 2, 128], BF16); KTIL = sb("KTIL", [128, 2, 128], BF16)
        BENDF = sb("BENDF", [128, 2, 128], BF16); KENDF = sb("KENDF", [128, 2, 128], BF16)
        BENDT = sb("BENDT", [128, 2, 128], BF16); KENDT = sb("KENDT", [128, 2, 128], BF16)
        RSC = sb("RSC", [128, 16])
        PQ = [[sb("PQ%d_%d" % (h, i), [128, 256], BF16) for i in range(2)] for h in range(4)]
        TTf = [sb("TTf%d" % h, [128, 128]) for h in range(4)]
        TTb = [sb("TTb%d" % h, [128, 128], BF16) for h in range(4)]
        AKB = [sb("AKB%d" % h, [128, 384], BF16) for h in range(4)]
        Wb = [sb("Wb%d" % h, [128, 64], BF16) for h in range(4)]
        Ub = [sb("Ub%d" % h, [128, 64], BF16) for h in range(4)]
        XS = sb("XS", [128, 2, 64]); XSb = sb("XSb", [128, 2, 64], BF16)
        YY = sb("YY", [128, 2, 128]); DD = sb("DD", [128, 2, 128]); RST = sb("RST", [128, 2, 128])
        LG = sb("LG", [128, 128]); CSG = sb("CSG", [128, 128]); EQ = sb("EQ", [128, 128]); EK = sb("EK", [128, 128]); EKE = sb("EKE", [128, 128])
        QE = sb("QE", [128, 128], BF16); KE = sb("KE", [128, 128], BF16); KEP = sb("KEP", [128, 4, 128], BF16)
        KEF = sb("KEF", [128, 128], BF16); KETM = sb("KETM", [128, 128], BF16)
        PTG = sb("PTG", [128, 512], BF16)
        GS = sb("GS", [128, 256]); GSb = sb("GSb", [128, 256], BF16); GTMP = sb("GTMP", [128, 256])
        GSC = sb("GSC", [128, 8])
        ADT = sb("ADT", [128, 24])

        cst = lambda a, b: CST[:, a:b]
        IDf = cst(K_ID, K_ID + 128)
        ONESf = cst(K_ONES, K_ONES + 128)

        def dump(name, ap, reads, shape, dt=F32):
            if debug is None or name.split("_")[0] not in debug:
                return
            o = nc.dram_tensor("dbg_" + name, list(shape), dt, kind="ExternalOutput").ap()
            dbg_out[name] = o
            S.add("sp", lambda e: e.dma_start(out=o, in_=ap), reads=reads, dma_key="dbg")

        for a in range(0, K_END, 1024):
            b = min(K_END, a + 1024)
            S.add("sp", lambda e, a=a, b=b: e.dma_start(out=CST[:, a:b], in_=cst_d[:, a:b]), writes=["CST"], dma_key="init")
        S.barrier()
        S.add("dve", lambda e: e.tensor_copy(out=IDb[:], in_=IDf), reads=["CST"], writes=["IDb"])
        S.add("dve", lambda e: e.tensor_scalar(out=BD64S[:], in0=cst(K_BD64, K_BD64 + 128), scalar1=1.0 / 64, scalar2=None, op0=ALU.mult), reads=["CST"], writes=["BD64S"])
        for i in range(2):
            S.add("pool", lambda e, i=i: e.memset(HT[i][:], 0.0), writes=["HT%d" % i])
            S.add("pool", lambda e, i=i: e.memset(KT[i][:], 0.0), writes=["KT%d" % i])
            S.add("pool", lambda e, i=i: e.memset(AV[i][:], 0.0), writes=["AV%d" % i])
        S.add("pool", lambda e: e.memset(W2P[:], 0.0), writes=["W2P"])
        S.add("pool", lambda e: e.memset(A2P[:], 0.0), writes=["A2P"])

        stg_ctr = [0]

        for l in range(L):
            S.barrier()
            S.add("sp", lambda e, l=l: e.dma_start(out=PCOL[:], in_=pcol_d[l]), writes=["PCOL"], dma_key="pl")
            S.add("sp", lambda e, l=l: e.dma_start(out=MUR[:, 0:768], in_=murkv_d[l:l + 1, :].partition_broadcast(128)), writes=["XN"], dma_key="pl")
            S.add("sp", lambda e, l=l: e.dma_start(out=LST[:, :, 0:64], in_=w1_d[l].rearrange("(k p) n -> p k n", p=128)), writes=["LST"], dma_key="pl")
            S.add("sp", lambda e, l=l: e.dma_start(out=LST[:, :, 64:128], in_=a1_d[l].rearrange("(k p) n -> p k n", p=128)), writes=["LST"], dma_key="pl")
            S.add("sp", lambda e, l=l: e.dma_start(out=LST[:, :, 128:144], in_=gk1_d[l].rearrange("(k p) n -> p k n", p=128)), writes=["LST"], dma_key="pl")
            S.add("sp", lambda e, l=l: e.dma_start(out=L2ST[0:64, :], in_=w2_d[l]), writes=["L2ST"], dma_key="pl")
            S.add("sp", lambda e, l=l: e.dma_start(out=L2ST[64:128, :], in_=a2_d[l]), writes=["L2ST"], dma_key="pl")
            S.add("sp", lambda e, l=l: e.dma_start(out=G2ST[:], in_=gk2_d[l]), writes=["G2ST"], dma_key="pl")
            S.barrier()
            S.add("dve", lambda e: e.tensor_scalar(out=OMMUR[:, 0:768], in0=MUR[:, 0:768], scalar1=-1.0, scalar2=1.0, op0=ALU.mult, op1=ALU.add), reads=["XN"], writes=["XO0"])
            S.add("pool", lambda e: e.tensor_copy(out=W2P[0:64, :], in_=L2ST[0:64, :]), reads=["L2ST"], writes=["W2P"])
            S.add("pool", lambda e: e.tensor_copy(out=A2P[64:128, :], in_=L2ST[64:128, :]), reads=["L2ST"], writes=["A2P"])
            S.add("pool", lambda e: e.tensor_copy(out=GK2[:], in_=G2ST[:]), reads=["G2ST"], writes=["GK2"])
            S.add("pool", lambda e: e.tensor_copy(out=WG1[:], in_=LST[:, :, 128:144]), reads=["LST"], writes=["WG1"])
            S.add("dve", lambda e: e.tensor_scalar(out=SC[:, 0:16], in0=PCOL[:, C_MUW:C_MUW + 16], scalar1=-1.0, scalar2=1.0, op0=ALU.mult, op1=ALU.add), reads=["PCOL"], writes=["SC"])
            for k in range(8):
                S.add("dve", lambda e, k=k: e.tensor_scalar(out=WL1A[:, k, 0:64], in0=LST[:, k, 0:64], scalar1=SC[:, k:k + 1], scalar2=None, op0=ALU.mult), reads=["LST", "SC"], writes=["WL1A"])
                S.add("dve", lambda e, k=k: e.tensor_scalar(out=WL1A[:, k, 64:128], in0=LST[:, k, 64:128], scalar1=SC[:, 8 + k:9 + k], scalar2=None, op0=ALU.mult), reads=["LST", "SC"], writes=["WL1A"])
                S.add("dve", lambda e, k=k: e.tensor_scalar(out=WL1B[:, k, 0:64], in0=LST[:, k, 0:64], scalar1=PCOL[:, C_MUW + k:C_MUW + k + 1], scalar2=None, op0=ALU.mult), reads=["LST", "PCOL"], writes=["WL1B"])
                S.add("dve", lambda e, k=k: e.tensor_scalar(out=WL1B[:, k, 64:128], in0=LST[:, k, 64:128], scalar1=PCOL[:, C_MUA + k:C_MUA + k + 1], scalar2=None, op0=ALU.mult), reads=["LST", "PCOL"], writes=["WL1B"])
            S.add("act", lambda e: e.activation(out=CACT[:], in_=PCOL[:, C_C:C_C + 8], func=AF.Silu), reads=["PCOL"], writes=["CACT"])
            S.add("dve", lambda e: e.tensor_copy(out=ADT[:], in_=PCOL[:, C_ADAB:C_ADAB + 24]), reads=["PCOL"], writes=["ADT"])
            for k in range(8):
                pb, pbn = nb()
                for hf in range(2):
                    i = stg_ctr[0] % 2
                    stg_ctr[0] += 1
                    st = STG[i]
                    sn = "STG%d" % i
                    S.add("sp", lambda e, st=st, l=l, k=k, hf=hf: e.dma_start(out=st[:, :], in_=adaw_d[l, k * 128:(k + 1) * 128, hf * 1536:(hf + 1) * 1536]), writes=[sn], dma_key=("stg", i))
                    for jj in range(12):
                        j = hf * 12 + jj
                        S.add("pe", lambda e, st=st, k=k, pb=pb, j=j, jj=jj: e.matmul(pb[:, j:j + 1], lhsT=st[:, jj * 128:(jj + 1) * 128], rhs=CACT[:, k:k + 1], start=True, stop=True),
                              reads=[sn, "CACT"], writes=[pbn])
                S.add("dve", lambda e, pb=pb: e.tensor_tensor(out=ADT[:], in0=pb[:, 0:24], in1=ADT[:], op=ALU.add), reads=["ADT"], writes=[pbn, "ADT"])
            S.add("dve", lambda e: e.tensor_copy(out=PC2[:, 0:24], in_=ADT[:]), reads=["ADT"], writes=["PC2"])
            S.add("dve", lambda e: e.scalar_tensor_tensor(out=PC2[:, 24:32], in0=PC2[:, 8:16], scalar=1.0, in1=PCOL[:, C_GPRE:C_GPRE + 8], op0=ALU.add, op1=ALU.mult), reads=["PC2", "PCOL"], writes=["PC2"])
            S.add("dve", lambda e: e.tensor_scalar(out=PC2[:, 32:33], in0=PCOL[:, C_GKB:C_GKB + 1], scalar1=-1.0, scalar2=None, op0=ALU.mult), reads=["PCOL"], writes=["PC2"])
            S.add("dve", lambda e: e.tensor_tensor(out=PC2[:, 33:41], in0=PC2[:, 16:24], in1=PCOL[:, C_GPOST:C_GPOST + 8], op=ALU.mult), reads=["PC2", "PCOL"], writes=["PC2"])
            for half in range(2):
                pb, pbn = nb()
                for cc in range(4):
                    c = half * 4 + cc
                    S.add("dve", lambda e, c=c: e.tensor_scalar(out=DG[:], in0=IDf, scalar1=PC2[:, 33 + c:34 + c], scalar2=None, op0=ALU.mult), reads=["CST", "PC2"], writes=["DG"])
                    S.add("pe", lambda e, pb=pb, cc=cc: e.matmul(pb[:, cc * 128:(cc + 1) * 128], lhsT=ONESf, rhs=DG[:], start=True, stop=True), reads=["CST", "DG"], writes=[pbn])
                S.add("act", lambda e, pb=pb, half=half: e.activation(out=GG[:, half * 512:(half + 1) * 512], in_=pb[:, :], func=AF.Copy), reads=[], writes=[pbn, "GG"])
            for k in range(8):
                for hf in range(2):
                    i = stg_ctr[0] % 2
                    stg_ctr[0] += 1
                    st = STG[i]
                    sn = "STG%d" % i
                    S.add("sp", lambda e, st=st, l=l, k=k, hf=hf: e.dma_start(out=st[:, :], in_=win_d[l, k * 128:(k + 1) * 128, hf * 1536:(hf + 1) * 1536]), writes=[sn], dma_key=("stg", i))
                    if hf == 0:
                        S.add("act", lambda e, st=st, k=k: e.activation(out=WIN[:, k, 0:1280], in_=st[:, 0:1280], func=AF.Copy), reads=[sn], writes=["WIN"])
                        S.add("dve", lambda e, st=st, k=k: e.tensor_tensor(out=WIN[:, k, 1280:1536], in0=st[:, 1280:1536], in1=OMMUR[:, 0:256], op=ALU.mult), reads=[sn, "XO0"], writes=["WIN"])
                        S.add("dve", lambda e, st=st, k=k: e.tensor_tensor(out=WRB[:, k, 0:256], in0=st[:, 1280:1536], in1=MUR[:, 0:256], op=ALU.mult), reads=[sn, "XN"], writes=["WRB"])
                        for g in range(2):
                            for dup in range(2):
                                S.add("pool", lambda e, st=st, k=k, g=g, dup=dup: e.tensor_copy(out=WKD[:, k, g * 128 + dup * 64:g * 128 + dup * 64 + 64], in_=st[:, 512 + g * 64:512 + g * 64 + 64]),
                                      reads=[sn], writes=["WKD"])
                    else:
                        S.add("dve", lambda e, st=st, k=k: e.tensor_tensor(out=WIN[:, k, 1536:2048], in0=st[:, 0:512], in1=OMMUR[:, 256:768], op=ALU.mult), reads=[sn, "XO0"], writes=["WIN"])
                        S.add("dve", lambda e, st=st, k=k: e.tensor_tensor(out=WRB[:, k, 256:768], in0=st[:, 0:512], in1=MUR[:, 256:768], op=ALU.mult), reads=[sn, "XN"], writes=["WRB"])
                        S.add("act", lambda e, st=st, k=k: e.activation(out=WIN[:, k, 2048:3072], in_=st[:, 512:1536], func=AF.Copy), reads=[sn], writes=["WIN"])
            for k in range(8):
                i = stg_ctr[0] % 2
                stg_ctr[0] += 1
                st = STG[i]
                sn = "STG%d" % i
                S.add("sp", lambda e, st=st, l=l, k=k: e.dma_start(out=st[:, 0:D], in_=wout_d[l, k * 128:(k + 1) * 128, :]), writes=[sn], dma_key=("stg", i))
                S.add("act", lambda e, st=st, k=k: e.activation(out=WOUT[:, k, :], in_=st[:, 0:D], func=AF.Copy), reads=[sn], writes=["WOUT"])
            S.add("pool", lambda e: e.memset(XS[:], 0.0), writes=["XS"])
            S.add("pool", lambda e: e.memset(XSb[:], 0.0), writes=["XSb"])
            S.add("pool", lambda e: e.memset(GS[:], 0.0), writes=["GS"])
            S.add("pool", lambda e: e.memset(GSb[:], 0.0), writes=["GSb"])
            S.barrier()

            src = x_d if l == 0 else xs_d[(l - 1) % 2]
            dst = y_d if l == L - 1 else xs_d[l % 2]

            def load_x(t, src=src):
                p = t % 2
                S.add("sp", lambda e: e.dma_start(out=XT[p][:], in_=src[t * 128:(t + 1) * 128, :]), writes=["XT%d" % p], dma_key=("xl", p))

            load_x(0)
            for t in range(NT):
                p = t % 2
                q = 1 - p
                if t + 1 < NT:
                    load_x(t + 1)
                xt, xtn = XT[p], "XT%d" % p
                ht, htn = HT[p], "HT%d" % p
                S.add("act", lambda e, xt=xt: e.activation(out=XN[:], in_=xt[:], func=AF.Square, accum_out=SC[:, 0:1]), reads=[xtn], writes=["XN", "SC"])
                S.add("act", lambda e: e.activation(out=SC[:, 1:2], in_=SC[:, 0:1], func=AF.Ln, scale=1.0 / D, bias=1e-6), reads=["SC"], writes=["SC"])
                S.add("act", lambda e: e.activation(out=SC[:, 2:3], in_=SC[:, 1:2], func=AF.Exp, scale=-0.5), reads=["SC"], writes=["SC"])
                S.add("dve", lambda e, xt=xt: e.tensor_scalar(out=XN[:], in0=xt[:], scalar1=SC[:, 2:3], scalar2=None, op0=ALU.mult), reads=[xtn, "SC"], writes=["XN"])
                if t == 0:
                    S.add("pool", lambda e, ht=ht: e.memset(ht[:, :, 0:1], 0.0), writes=[htn])
                else:
                    S.add("pool", lambda e, ht=ht, q=q: e.tensor_copy(out=ht[:, :, 0:1], in_=HT[q][:, :, 128:129]), reads=["HT%d" % q], writes=[htn])
                for half in range(2):
                    pb, pbn = nb()
                    for cc in range(4):
                        c = half * 4 + cc
                        S.add("pe", lambda e, pb=pb, cc=cc, c=c: e.transpose(pb[:, cc * 128:(cc + 1) * 128], in_=XN[:, c * 128:(c + 1) * 128], identity=IDf), reads=["XN", "CST"], writes=[pbn])
                    for cc in range(4):
                        c = half * 4 + cc
                        S.add("act", lambda e, pb=pb, cc=cc, c=c, ht=ht: e.activation(out=ht[:, c, 1:129], in_=pb[:, cc * 128:(cc + 1) * 128], func=AF.Identity,
                                                                                 scale=PC2[:, 24 + c:25 + c], bias=PC2[:, c:c + 1]), reads=["PC2"], writes=[pbn, htn])
                hc = lambda k, ht=ht: ht[:, k, 1:129]
                hp = lambda k, ht=ht: ht[:, k, 0:128]

                def proj_fm(out_ap, pbn, wfn, M, shift_wfn=None):
                    n = 16 if shift_wfn is not None else 8
                    i = 0
                    for k in range(8):
                        S.add("pe", lambda e, k=k, i=i, hk=hc(k): e.matmul(out_ap, lhsT=wfn(k), rhs=hk, start=(i == 0), stop=(i == n - 1)), reads=[htn, "WIN", "WRB", "WKD", "WL1A", "WL1B", "WG1"], writes=[pbn])
                        i += 1
                        if shift_wfn is not None:
                            S.add("pe", lambda e, k=k, i=i, hk=hp(k): e.matmul(out_ap, lhsT=shift_wfn(k), rhs=hk, start=False, stop=(i == n - 1)), reads=[htn, "WIN", "WRB", "WL1B"], writes=[pbn])
                            i += 1

                pb, pbn = nb()
                for c in range(4):
                    proj_fm(pb[:, c * 128:(c + 1) * 128], pbn, lambda k, c=c: WIN[:, k, c * 128:(c + 1) * 128], 128)
                S.add("act", lambda e, pb=pb: e.activation(out=QT[:], in_=pb[:, :], func=AF.Copy, scale=0.125), writes=[pbn, "QT"])
                pb, pbn = nb()
                for g in range(2):
                    proj_fm(pb[:, g * 128:(g + 1) * 128], pbn, lambda k, g=g: WKD[:, k, g * 128:(g + 1) * 128], 128)
                S.add("act", lambda e, pb=pb, p=p: e.activation(out=KT[p][:], in_=pb[:, 0:256], func=AF.Copy), writes=[pbn, "KT%d" % p])
                pb, pbn = nb()
                for c in range(4):
                    proj_fm(pb[:, c * 128:(c + 1) * 128], pbn, lambda k, c=c: WIN[:, k, 768 + c * 128:768 + (c + 1) * 128], 128)
                S.add("act", lambda e, pb=pb: e.activation(out=SAG[:], in_=pb[:, :], func=AF.Silu), writes=[pbn, "SAG"])
                pb, pbn = nb()
                for c in range(4):
                    proj_fm(pb[:, c * 128:(c + 1) * 128], pbn, lambda k, c=c: WIN[:, k, 1280 + c * 128:1280 + (c + 1) * 128], 128, lambda k, c=c: WRB[:, k, c * 128:(c + 1) * 128])
                S.add("dve", lambda e, pb=pb: e.tensor_copy(out=RKV[:, 0:4, :], in_=pb[:, :].rearrange("p (c n) -> p c n", c=4)), writes=[pbn, "RKV"])
                pb, pbn = nb()
                for c in range(4, 6):
                    proj_fm(pb[:, (c - 4) * 128:(c - 3) * 128], pbn, lambda k, c=c: WIN[:, k, 1280 + c * 128:1280 + (c + 1) * 128], 128, lambda k, c=c: WRB[:, k, c * 128:(c + 1) * 128])
                for j in range(2):
                    proj_fm(pb[:, (2 + j) * 128:(3 + j) * 128], pbn, lambda k, j=j: WIN[:, k, 2304 + j * 128:2304 + (j + 1) * 128], 128)
                S.add("dve", lambda e, pb=pb: e.tensor_copy(out=RKV[:, 4:6, :], in_=pb[:, 0:256].rearrange("p (c n) -> p c n", c=2)), writes=[pbn, "RKV"])
                S.add("act", lambda e, pb=pb: e.activation(out=GQK[:], in_=pb[:, 256:512], func=AF.Copy), writes=[pbn, "GQK"])
                pb, pbn = nb()
                for j, off in enumerate((2048, 2176, 2816, 2944)):
                    proj_fm(pb[:, j * 128:(j + 1) * 128], pbn, lambda k, off=off: WIN[:, k, off:off + 128], 128)
                S.add("act", lambda e, pb=pb: e.activation(out=SRGG[:], in_=pb[:, :], func=AF.Silu), writes=[pbn, "SRGG"])
                pb, pbn = nb()
                proj_fm(pb[:, 0:128], pbn, lambda k: WL1A[:, k, :], 128, lambda k: WL1B[:, k, :])
                proj_fm(pb[0:16, 128:256], pbn, lambda k: WG1[:, k, :], 16)
                S.add("act", lambda e, pb=pb: e.activation(out=L1b[0:64, :], in_=pb[0:64, 0:128], func=AF.Tanh), writes=[pbn, "L1b"])
                S.add("act", lambda e, pb=pb: e.activation(out=L1b[64:128, :], in_=pb[64:128, 0:128], func=AF.Copy), writes=[pbn, "L1b"])
                S.add("act", lambda e, pb=pb: e.activation(out=G1b[:], in_=pb[0:16, 128:256], func=AF.Copy), writes=[pbn, "G1b"])
                pb, pbn = nb()
                for k in range(8):
                    S.add("pe", lambda e, k=k, pb=pb, hk=hc(k): e.matmul(pb[:, 0:128], lhsT=hk, rhs=WIN[:, k, 640:768], start=(k == 0), stop=(k == 7)), reads=[htn, "WIN"], writes=[pbn])
                i = 0
                for k in range(8):
                    S.add("pe", lambda e, k=k, pb=pb, i=i, hk=hc(k): e.matmul(pb[:, 128:384], lhsT=hk, rhs=WIN[:, k, 1792:2048], start=(i == 0), stop=False), reads=[htn, "WIN"], writes=[pbn])
                    i += 1
                    S.add("pe", lambda e, k=k, pb=pb, i=i, hk=hp(k): e.matmul(pb[:, 128:384], lhsT=hk, rhs=WRB[:, k, 512:768], start=False, stop=(i == 15)), reads=[htn, "WRB"], writes=[pbn])
                    i += 1
                S.add("act", lambda e, pb=pb, p=p: e.activation(out=AV[p][:], in_=pb[:, 0:128], func=AF.Copy), writes=[pbn, "AV%d" % p])
                S.add("dve", lambda e, pb=pb: e.tensor_copy(out=RVb[:], in_=pb[:, 128:384]), writes=[pbn, "RVb"])
                pb, pbn = nb()
                for k in range(8):
                    S.add("pe", lambda e, k=k, pb=pb, hk=hc(k): e.matmul(pb[:, 0:256], lhsT=hk, rhs=WIN[:, k, 2560:2816], start=(k == 0), stop=(k == 7)), reads=[htn, "WIN"], writes=[pbn])
                S.add("act", lambda e, pb=pb: e.activation(out=GVb[:], in_=pb[:, 0:256], func=AF.Copy), writes=[pbn, "GVb"])
                pb, pbn = nb()
                for c in range(2):
                    S.add("pe", lambda e, c=c, pb=pb: e.matmul(pb[:, c * 128:(c + 1) * 128], lhsT=W2P[:, c * 128:(c + 1) * 128], rhs=L1b[:], start=True, stop=True), reads=["W2P", "L1b"], writes=[pbn])
                    S.add("pe", lambda e, c=c, pb=pb: e.matmul(pb[:, (2 + c) * 128:(3 + c) * 128], lhsT=A2P[:, c * 128:(c + 1) * 128], rhs=L1b[:], start=True, stop=True), reads=["A2P", "L1b"], writes=[pbn])
                for c in range(2):
                    S.add("act", lambda e, c=c, pb=pb: e.activation(out=ELL[:, c, :], in_=pb[:, c * 128:(c + 1) * 128], func=AF.Sigmoid, bias=PCOL[:, C_W0 + c:C_W0 + c + 1]), reads=["PCOL"], writes=[pbn, "ELL"])
                    S.add("act", lambda e, c=c, pb=pb: e.activation(out=AA[:, c, :], in_=pb[:, (2 + c) * 128:(3 + c) * 128], func=AF.Sigmoid, bias=PCOL[:, C_A0 + c:C_A0 + c + 1]), reads=["PCOL"], writes=[pbn, "AA"])
                pb, pbn = nb()
                S.add("pe", lambda e, pb=pb: e.matmul(pb[:, 0:128], lhsT=GK2[:], rhs=G1b[:], start=True, stop=True), reads=["GK2", "G1b"], writes=[pbn])
                S.add("act", lambda e, pb=pb: e.activation(out=LG[:], in_=pb[:, 0:128], func=AF.Exp, scale=-1.0, bias=PC2[:, 32:33]), reads=["PC2"], writes=[pbn, "LG"])
                S.add("act", lambda e: e.activation(out=LG[:], in_=LG[:], func=AF.Ln, bias=1.0), reads=["LG"], writes=["LG"])

                abk = K_AB0 if t == 0 else K_AB
                for c in range(4):
                    for hh in range(2):
                        h = 2 * c + hh
                        g = h // 4
                        hs = 64 * hh
                        a = h % 2
                        ss, pp, dgb, ptn, asc = SS[a], PP[a], DGb[a], PTN[a], ASC[a]
                        ssn, ppn, dgn, ptnn, ascn = "SS%d" % a, "PP%d" % a, "DGb%d" % a, "PTN%d" % a, "ASC%d" % a
                        pb, pbn = nb()
                        S.add("pe", lambda e, pb=pb, c=c, hs=hs, g=g, q=q: e.matmul(pb[:, 0:128], lhsT=QT[hs:hs + 64, c * 128:(c + 1) * 128], rhs=KT[q][hs:hs + 64, g * 128:(g + 1) * 128], start=True, stop=True),
                              reads=["QT", "KT%d" % q], writes=[pbn])
                        S.add("pe", lambda e, pb=pb, c=c, hs=hs, g=g, p=p: e.matmul(pb[:, 128:256], lhsT=QT[hs:hs + 64, c * 128:(c + 1) * 128], rhs=KT[p][hs:hs + 64, g * 128:(g + 1) * 128], start=True, stop=True),
                              reads=["QT", "KT%d" % p], writes=[pbn])
                        S.add("dve", lambda e, pb=pb, ss=ss, h=h, abk=abk: e.scalar_tensor_tensor(out=ss[:], in0=CST[:, abk:abk + 256], scalar=-SLOPES[h], in1=pb[:, 0:256], op0=ALU.mult, op1=ALU.add), reads=["CST"], writes=[pbn, ssn])
                        S.add("dve", lambda e, ss=ss, asc=asc: e.reduce_max(out=asc[:, 0:1], in_=ss[:], axis=AX.X), reads=[ssn], writes=[ascn])
                        S.add("dve", lambda e, asc=asc: e.tensor_scalar(out=asc[:, 1:2], in0=asc[:, 0:1], scalar1=-1.0, scalar2=None, op0=ALU.mult), reads=[ascn], writes=[ascn])
                        S.add("act", lambda e, ss=ss, pp=pp, asc=asc: e.activation(out=pp[:], in_=ss[:], func=AF.Exp, bias=asc[:, 1:2], accum_out=asc[:, 2:3]), reads=[ssn, ascn], writes=[ppn, ascn])
                        S.add("act", lambda e, asc=asc, h=h: e.activation(out=asc[:, 3:4], in_=asc[:, 1:2], func=AF.Exp, bias=PCOL[:, C_SINK + h:C_SINK + h + 1]), reads=[ascn, "PCOL"], writes=[ascn])
                        S.add("dve", lambda e, asc=asc: e.tensor_tensor(out=asc[:, 4:5], in0=asc[:, 2:3], in1=asc[:, 3:4], op=ALU.add), reads=[ascn], writes=[ascn])
                        S.add("dve", lambda e, asc=asc: e.reciprocal(out=asc[:, 5:6], in_=asc[:, 4:5]), reads=[ascn], writes=[ascn])
                        S.add("dve", lambda e, asc=asc, dgb=dgb: e.tensor_scalar(out=dgb[:], in0=IDf, scalar1=asc[:, 5:6], scalar2=None, op0=ALU.mult), reads=[ascn, "CST"], writes=[dgn])
                        pb2, pbn2 = nb()
                        for j in range(2):
                            S.add("pe", lambda e, pb2=pb2, pp=pp, dgb=dgb, j=j: e.matmul(pb2[:, j * 128:(j + 1) * 128], lhsT=pp[:, j * 128:(j + 1) * 128], rhs=dgb[:], start=True, stop=True), reads=[ppn, dgn], writes=[pbn2])
                        S.add("act", lambda e, pb2=pb2, ptn=ptn: e.activation(out=ptn[:], in_=pb2[:, 0:256], func=AF.Copy), writes=[pbn2, ptnn])
                    pb, pbn = nb()
                    for hh in range(2):
                        h = 2 * c + hh
                        g = h // 4
                        hs = 64 * hh
                        a = h % 2
                        S.add("pe", lambda e, pb=pb, hs=hs, g=g, a=a, q=q: e.matmul(pb[hs:hs + 64, 0:128], lhsT=AV[q][:, g * 64:(g + 1) * 64], rhs=PTN[a][:, 0:128], start=True, stop=False), reads=["AV%d" % q, "PTN%d" % a], writes=[pbn])
                        S.add("pe", lambda e, pb=pb, hs=hs, g=g, a=a, p=p: e.matmul(pb[hs:hs + 64, 0:128], lhsT=AV[p][:, g * 64:(g + 1) * 64], rhs=PTN[a][:, 128:256], start=False, stop=True), reads=["AV%d" % p, "PTN%d" % a], writes=[pbn])
                    S.add("dve", lambda e, pb=pb, c=c: e.tensor_tensor(out=CAT[:, c, :], in0=pb[:, 0:128], in1=SAG[:, c * 128:(c + 1) * 128], op=ALU.mult), reads=["SAG"], writes=[pbn, "CAT"])

                R = lambda c: RKV[:, c, :]
                Kr = lambda c: RKV[:, 2 + c, :]
                Vr = lambda c: RKV[:, 4 + c, :]
                for c in range(2):
                    col = lambda base, c=c: PCOL[:, base + c:base + c + 1]
                    S.add("dve", lambda e, c=c: e.tensor_tensor_scan(out=CS[:, c, :], data0=ONESf, data1=ELL[:, c, :], initial=0.0, op0=ALU.mult, op1=ALU.add), reads=["CST", "ELL"], writes=["CS"])
                    S.add("act", lambda e, c=c: e.activation(out=EP[:, c, :], in_=CS[:, c, :], func=AF.Exp, scale=-DECAY_K), reads=["CS"], writes=["EP"])
                    S.add("act", lambda e, c=c: e.activation(out=EI[:, c, :], in_=CS[:, c, :], func=AF.Exp, scale=DECAY_K), reads=["CS"], writes=["EI"])
                    S.add("dve", lambda e, c=c: e.tensor_tensor(out=TMP[:, c, :], in0=CS[:, c, :], in1=ELL[:, c, :], op=ALU.subtract), reads=["CS", "ELL"], writes=["TMP"])
                    S.add("act", lambda e, c=c: e.activation(out=EPV[:, c, :], in_=TMP[:, c, :], func=AF.Exp, scale=-DECAY_K), reads=["TMP"], writes=["EPV"])
                    S.add("dve", lambda e, c=c: e.tensor_scalar(out=RSC[:, c:c + 1], in0=CS[:, c, 127:128], scalar1=-DECAY_K, scalar2=None, op0=ALU.mult), reads=["CS"], writes=["RSC"])
                    S.add("act", lambda e, c=c: e.activation(out=EE[:, c, :], in_=CS[:, c, :], func=AF.Exp, scale=DECAY_K, bias=RSC[:, c:c + 1]), reads=["CS", "RSC"], writes=["EE"])
                    S.add("act", lambda e, c=c: e.activation(out=RSC[:, 2 + c:3 + c], in_=RSC[:, c:c + 1], func=AF.Exp), reads=["RSC"], writes=["RSC"])
                    S.add("dve", lambda e, c=c, cc_=col(C_KK): e.tensor_scalar(out=KKt[:, c, :], in0=Kr(c), scalar1=cc_, scalar2=None, op0=ALU.mult), reads=["RKV", "PCOL"], writes=["KKt"])
                    S.add("act", lambda e, c=c: e.activation(out=KK2[:, c, :], in_=KKt[:, c, :], func=AF.Square), reads=["KKt"], writes=["KK2"])
                    pb, pbn = nb()
                    S.add("pe", lambda e, c=c, pb=pb: e.matmul(pb[:, 0:128], lhsT=cst(K_BD64, K_BD64 + 128), rhs=KK2[:, c, :], start=True, stop=True), reads=["CST", "KK2"], writes=[pbn])
                    S.add("dve", lambda e, c=c, pb=pb: e.tensor_scalar(out=TMP[:, c, :], in0=pb[:, 0:128], scalar1=1e-18, scalar2=None, op0=ALU.max), reads=[], writes=[pbn, "TMP"])
                    S.add("act", lambda e, c=c: e.activation(out=TMP[:, c, :], in_=TMP[:, c, :], func=AF.Ln), reads=["TMP"], writes=["TMP"])
                    S.add("act", lambda e, c=c: e.activation(out=TMP[:, c, :], in_=TMP[:, c, :], func=AF.Exp, scale=-0.5), reads=["TMP"], writes=["TMP"])
                    S.add("dve", lambda e, c=c: e.tensor_tensor(out=KKN[:, c, :], in0=KKt[:, c, :], in1=TMP[:, c, :], op=ALU.mult), reads=["KKt", "TMP"], writes=["KKN"])
                    S.add("dve", lambda e, c=c, cc_=col(C_KA): e.tensor_scalar(out=KP[:, c, :], in0=AA[:, c, :], scalar1=-1.0, scalar2=cc_, op0=ALU.add, op1=ALU.mult), reads=["AA", "PCOL"], writes=["KP"])
                    S.add("dve", lambda e, c=c: e.scalar_tensor_tensor(out=KP[:, c, :], in0=KP[:, c, :], scalar=1.0, in1=Kr(c), op0=ALU.add, op1=ALU.mult), reads=["KP", "RKV"], writes=["KP"])
                    S.add("dve", lambda e, c=c: e.scalar_tensor_tensor(out=AR[:, c, 0:128], in0=KKN[:, c, :], scalar=-1.0, in1=EPV[:, c, :], op0=ALU.mult, op1=ALU.mult), reads=["KKN", "EPV"], writes=["AR"])
                    S.add("dve", lambda e, c=c: e.tensor_tensor(out=AR[:, c, 128:256], in0=R(c), in1=EP[:, c, :], op=ALU.mult), reads=["RKV", "EP"], writes=["AR"])
                    S.add("dve", lambda e, c=c: e.tensor_tensor(out=TKA[:, c, :], in0=KKN[:, c, :], in1=AA[:, c, :], op=ALU.mult), reads=["KKN", "AA"], writes=["TKA"])
                    S.add("dve", lambda e, c=c: e.tensor_tensor(out=BTIL[:, c, :], in0=TKA[:, c, :], in1=EI[:, c, :], op=ALU.mult), reads=["TKA", "EI"], writes=["BTIL"])
                    S.add("dve", lambda e, c=c: e.tensor_tensor(out=BENDF[:, c, :], in0=TKA[:, c, :], in1=EE[:, c, :], op=ALU.mult), reads=["TKA", "EE"], writes=["BENDF"])
                    S.add("dve", lambda e, c=c: e.tensor_tensor(out=KTIL[:, c, :], in0=KP[:, c, :], in1=EI[:, c, :], op=ALU.mult), reads=["KP", "EI"], writes=["KTIL"])
                    S.add("dve", lambda e, c=c: e.tensor_tensor(out=KENDF[:, c, :], in0=KP[:, c, :], in1=EE[:, c, :], op=ALU.mult), reads=["KP", "EE"], writes=["KENDF"])
                    S.add("dve", lambda e, c=c, cc_=col(C_RK): e.scalar_tensor_tensor(out=PRD[:, c, :], in0=R(c), scalar=cc_, in1=KP[:, c, :], op0=ALU.mult, op1=ALU.mult), reads=["RKV", "PCOL", "KP"], writes=["PRD"])
                    pb, pbn = nb()
                    S.add("pe", lambda e, c=c, pb=pb: e.matmul(pb[:, 0:128], lhsT=cst(K_BD64, K_BD64 + 128), rhs=PRD[:, c, :], start=True, stop=True), reads=["CST", "PRD"], writes=[pbn])
                    S.add("dve", lambda e, c=c, pb=pb: e.tensor_tensor(out=BON[:, c, :], in0=pb[:, 0:128], in1=Vr(c), op=ALU.mult), reads=["RKV"], writes=[pbn, "BON"])
                    pbt = es.enter_context(nc.psum_tensor("pbt_%d_%d_%d" % (l, t, c), [128, 1024], BF16)) if False else None
                pb, pbn = nb()
                pbb = pb[:, :].bitcast(BF16)
                for c in range(2):
                    S.add("pe", lambda e, c=c, pbb=pbb: e.transpose(pbb[:, c * 128:(c + 1) * 128], in_=BENDF[:, c, :], identity=IDb[:]), reads=["BENDF", "IDb"], writes=[pbn])
                    S.add("pe", lambda e, c=c, pbb=pbb: e.transpose(pbb[:, (2 + c) * 128:(3 + c) * 128], in_=KENDF[:, c, :], identity=IDb[:]), reads=["KENDF", "IDb"], writes=[pbn])
                S.add("act", lambda e, pbb=pbb: e.activation(out=BENDT[:, :, :], in_=pbb[:, 0:256].rearrange("p (c n) -> p c n", c=2), func=AF.Copy), writes=[pbn, "BENDT"])
                S.add("act", lambda e, pbb=pbb: e.activation(out=KENDT[:, :, :], in_=pbb[:, 256:512].rearrange("p (c n) -> p c n", c=2), func=AF.Copy), writes=[pbn, "KENDT"])
                for h in range(4):
                    c, hs = h // 2, 64 * (h % 2)
                    pq0, pqn0 = PQ[h][0], "PQ%d_0" % h
                    pb, pbn = nb()
                    S.add("pe", lambda e, pb=pb, c=c, hs=hs: e.matmul(pb[:, 0:256], lhsT=BTIL[hs:hs + 64, c, :], rhs=AR[hs:hs + 64, c, :], start=True, stop=True), reads=["BTIL", "AR"], writes=[pbn])
                    S.add("pe", lambda e, pb=pb, c=c, hs=hs: e.matmul(pb[:, 256:384], lhsT=AR[hs:hs + 64, c, 0:128], rhs=BTIL[hs:hs + 64, c, :], start=True, stop=True), reads=["BTIL", "AR"], writes=[pbn])
                    S.add("dve", lambda e, pb=pb, pq0=pq0: e.tensor_tensor(out=pq0[:, 0:128], in0=pb[:, 0:128], in1=cst(K_MU2, K_MU2 + 128), op=ALU.mult), reads=["CST"], writes=[pbn, pqn0])
                    S.add("dve", lambda e, pb=pb, h=h: e.tensor_tensor(out=AKB[h][:, 128:256], in0=pb[:, 128:256], in1=cst(K_MU2 + 128, K_MU2 + 256), op=ALU.mult), reads=["CST"], writes=[pbn, "AKB%d" % h])
                    S.add("dve", lambda e, pb=pb, pq0=pq0: e.tensor_tensor(out=pq0[:, 128:256], in0=pb[:, 256:384], in1=cst(K_MLS, K_MLS + 128), op=ALU.mult), reads=["CST"], writes=[pbn, pqn0])
                    pb, pbn = nb()
                    S.add("pe", lambda e, pb=pb, c=c, hs=hs: e.matmul(pb[:, 0:256], lhsT=KTIL[hs:hs + 64, c, :], rhs=AR[hs:hs + 64, c, :], start=True, stop=True), reads=["KTIL", "AR"], writes=[pbn])
                    S.add("dve", lambda e, pb=pb, h=h: e.tensor_tensor(out=AKB[h][:, 0:128], in0=pb[:, 0:128], in1=cst(K_MU2, K_MU2 + 128), op=ALU.mult), reads=["CST"], writes=[pbn, "AKB%d" % h])
                    S.add("dve", lambda e, pb=pb, h=h: e.tensor_tensor(out=AKB[h][:, 256:384], in0=pb[:, 128:256], in1=cst(K_MU2 + 128, K_MU2 + 256), op=ALU.mult), reads=["CST"], writes=[pbn, "AKB%d" % h])
                    S.add("dve", lambda e, h=h, pq0=pq0: e.tensor_tensor(out=TTf[h][:], in0=pq0[:, 0:128], in1=IDf, op=ALU.add), reads=[pqn0, "CST"], writes=["TTf%d" % h])
                    S.add("pool", lambda e, h=h: e.tensor_copy(out=TTb[h][:], in_=TTf[h][:]), reads=["TTf%d" % h], writes=["TTb%d" % h])
                for lev in range(1, 7):
                    for h in range(4):
                        src_, srcn = PQ[h][(lev - 1) % 2], "PQ%d_%d" % (h, (lev - 1) % 2)
                        dst_, dstn = PQ[h][lev % 2], "PQ%d_%d" % (h, lev % 2)
                        pb, pbn = nb()
                        S.add("pe", lambda e, pb=pb, src_=src_: e.matmul(pb[:, 128:256], lhsT=src_[:, 0:128], rhs=src_[:, 128:256], start=True, stop=True), reads=[srcn], writes=[pbn])
                        if lev < 6:
                            S.add("pe", lambda e, pb=pb, src_=src_: e.matmul(pb[:, 0:128], lhsT=src_[:, 128:256], rhs=src_[:, 0:128], start=True, stop=True), reads=[srcn], writes=[pbn])
                            S.add("act", lambda e, pb=pb, dst_=dst_: e.activation(out=dst_[:, :], in_=pb[:, 0:256], func=AF.Copy), writes=[pbn, dstn])
                        else:
                            S.add("act", lambda e, pb=pb, dst_=dst_: e.activation(out=dst_[:, 128:256], in_=pb[:, 128:256], func=AF.Copy), writes=[pbn, dstn])
                        pb2, pbn2 = nb()
                        S.add("pe", lambda e, pb2=pb2, dst_=dst_, h=h: e.matmul(pb2[:, 0:128], lhsT=dst_[:, 128:256], rhs=TTb[h][:], start=True, stop=True), reads=[dstn, "TTb%d" % h], writes=[pbn2])
                        S.add("dve", lambda e, pb2=pb2, h=h: e.tensor_tensor(out=TTf[h][:], in0=pb2[:, 0:128], in1=TTf[h][:], op=ALU.add), reads=["TTf%d" % h], writes=[pbn2, "TTf%d" % h])
                        S.add("pool", lambda e, h=h: e.tensor_copy(out=TTb[h][:], in_=TTf[h][:]), reads=["TTf%d" % h], writes=["TTb%d" % h])
                for h in range(4):
                    c, hs, hh = h // 2, 64 * (h % 2), h % 2
                    vh = lambda h=h: RVb[:, h * 64:(h + 1) * 64]
                    pb, pbn = nb()
                    S.add("pe", lambda e, pb=pb, c=c, hs=hs: e.matmul(pb[:, 0:64], lhsT=AR[hs:hs + 64, c, 0:128], rhs=XSb[hs:hs + 64, c, :], start=True, stop=False), reads=["AR", "XSb"], writes=[pbn])
                    S.add("pe", lambda e, pb=pb, h=h, vh=vh: e.matmul(pb[:, 0:64], lhsT=AKB[h][:, 0:128], rhs=vh(), start=False, stop=True), reads=["AKB%d" % h, "RVb"], writes=[pbn])
                    S.add("act", lambda e, pb=pb, h=h: e.activation(out=Wb[h][:], in_=pb[:, 0:64], func=AF.Copy), writes=[pbn, "Wb%d" % h])
                    pb, pbn = nb()
                    S.add("pe", lambda e, pb=pb, h=h: e.matmul(pb[:, 0:64], lhsT=TTb[h][:], rhs=Wb[h][:], start=True, stop=True), reads=["TTb%d" % h, "Wb%d" % h], writes=[pbn])
                    S.add("act", lambda e, pb=pb, h=h: e.activation(out=Ub[h][:], in_=pb[:, 0:64], func=AF.Copy), writes=[pbn, "Ub%d" % h])
                for c in range(2):
                    pby, pbyn = nb()
                    pbx, pbxn = nb()
                    for hh in range(2):
                        h = 2 * c + hh
                        hs = 64 * hh
                        vh = lambda h=h: RVb[:, h * 64:(h + 1) * 64]
                        S.add("pe", lambda e, pby=pby, c=c, hs=hs: e.matmul(pby[hs:hs + 64, 0:128], lhsT=XSb[hs:hs + 64, c, :], rhs=AR[hs:hs + 64, c, 128:256], start=True, stop=False), reads=["XSb", "AR"], writes=[pbyn])
                        S.add("pe", lambda e, pby=pby, h=h, hs=hs: e.matmul(pby[hs:hs + 64, 0:128], lhsT=Ub[h][:], rhs=AKB[h][:, 128:256], start=False, stop=False), reads=["Ub%d" % h, "AKB%d" % h], writes=[pbyn])
                        S.add("pe", lambda e, pby=pby, h=h, hs=hs, vh=vh: e.matmul(pby[hs:hs + 64, 0:128], lhsT=vh(), rhs=AKB[h][:, 256:384], start=False, stop=True), reads=["RVb", "AKB%d" % h], writes=[pbyn])
                        S.add("pe", lambda e, pbx=pbx, h=h, hs=hs, c=c: e.matmul(pbx[hs:hs + 64, 0:64], lhsT=BENDT[:, c, hs:hs + 64], rhs=Ub[h][:], start=True, stop=False), reads=["BENDT", "Ub%d" % h], writes=[pbxn])
                        S.add("pe", lambda e, pbx=pbx, h=h, hs=hs, c=c, vh=vh: e.matmul(pbx[hs:hs + 64, 0:64], lhsT=KENDT[:, c, hs:hs + 64], rhs=vh(), start=False, stop=True), reads=["KENDT", "RVb"], writes=[pbxn])
                    S.add("dve", lambda e, pby=pby, c=c: e.tensor_copy(out=YY[:, c, :], in_=pby[:, 0:128]), writes=[pbyn, "YY"])
                    S.add("dve", lambda e, pbx=pbx, c=c: e.scalar_tensor_tensor(out=XS[:, c, :], in0=XS[:, c, :], scalar=RSC[:, 2 + c:3 + c], in1=pbx[:, 0:64], op0=ALU.mult, op1=ALU.add), reads=["XS", "RSC"], writes=[pbxn, "XS"])
                    S.add("pool", lambda e, c=c: e.tensor_copy(out=XSb[:, c, :], in_=XS[:, c, :]), reads=["XS"], writes=["XSb"])
                    if c == 1:
                        dump("YR_%d_%d" % (l, t), YY[:], ["YY"], [128, 2, 128])
                        dump("XS_%d_%d" % (l, t), XS[:], ["XS"], [128, 2, 64])
                        dump("TT_%d_%d" % (l, t), TTf[0][:], ["TTf0"], [128, 128])
                        dump("KKN_%d_%d" % (l, t), KKN[:], ["KKN"], [128, 2, 128])
                        dump("KP_%d_%d" % (l, t), KP[:], ["KP"], [128, 2, 128])
                        dump("UB_%d_%d" % (l, t), Ub[0][:], ["Ub0"], [128, 64], BF16)
                        dump("WB_%d_%d" % (l, t), Wb[0][:], ["Wb0"], [128, 64], BF16)
                        dump("AKB_%d_%d" % (l, t), AKB[0][:], ["AKB0"], [128, 384], BF16)
                    pb, pbn = nb()
                    S.add("pe", lambda e, pb=pb, c=c: e.matmul(pb[:, 0:128], lhsT=BD64S[:], rhs=YY[:, c, :], start=True, stop=True), reads=["BD64S", "YY"], writes=[pbn])
                    S.add("dve", lambda e, pb=pb, c=c: e.tensor_tensor(out=DD[:, c, :], in0=YY[:, c, :], in1=pb[:, 0:128], op=ALU.subtract), reads=["YY"], writes=[pbn, "DD"])
                    S.add("act", lambda e, c=c: e.activation(out=KK2[:, c, :], in_=DD[:, c, :], func=AF.Square), reads=["DD"], writes=["KK2"])
                    pb, pbn = nb()
                    S.add("pe", lambda e, pb=pb, c=c: e.matmul(pb[:, 0:128], lhsT=BD64S[:], rhs=KK2[:, c, :], start=True, stop=True), reads=["BD64S", "KK2"], writes=[pbn])
                    S.add("act", lambda e, pb=pb, c=c: e.activation(out=RST[:, c, :], in_=pb[:, 0:128], func=AF.Ln, bias=64e-5), writes=[pbn, "RST"])
                    S.add("act", lambda e, c=c: e.activation(out=RST[:, c, :], in_=RST[:, c, :], func=AF.Exp, scale=-0.5), reads=["RST"], writes=["RST"])
                    S.add("dve", lambda e, c=c: e.tensor_tensor(out=DD[:, c, :], in0=DD[:, c, :], in1=RST[:, c, :], op=ALU.mult), reads=["DD", "RST"], writes=["DD"])
                    S.add("dve", lambda e, c=c: e.tensor_scalar(out=DD[:, c, :], in0=DD[:, c, :], scalar1=PCOL[:, C_LNW + c:C_LNW + c + 1], scalar2=PCOL[:, C_LNB + c:C_LNB + c + 1], op0=ALU.mult, op1=ALU.add), reads=["DD", "PCOL"], writes=["DD"])
                    S.add("dve", lambda e, c=c: e.tensor_tensor(out=DD[:, c, :], in0=DD[:, c, :], in1=BON[:, c, :], op=ALU.add), reads=["DD", "BON"], writes=["DD"])
                    S.add("dve", lambda e, c=c: e.tensor_tensor(out=CAT[:, 4 + c, :], in0=DD[:, c, :], in1=SRGG[:, c * 128:(c + 1) * 128], op=ALU.mult), reads=["DD", "SRGG"], writes=["CAT"])

                S.add("dve", lambda e: e.tensor_tensor_scan(out=CSG[:], data0=ONESf, data1=LG[:], initial=0.0, op0=ALU.mult, op1=ALU.add), reads=["CST", "LG"], writes=["CSG"])
                S.add("act", lambda e: e.activation(out=EQ[:], in_=CSG[:], func=AF.Exp, scale=-1.0 / 16), reads=["CSG"], writes=["EQ"])
                S.add("act", lambda e: e.activation(out=EK[:], in_=CSG[:], func=AF.Exp, scale=1.0 / 16), reads=["CSG"], writes=["EK"])
                S.add("dve", lambda e: e.tensor_scalar(out=GSC[:, 0:1], in0=CSG[:, 127:128], scalar1=-1.0 / 16, scalar2=None, op0=ALU.mult), reads=["CSG"], writes=["GSC"])
                S.add("act", lambda e: e.activation(out=EKE[:], in_=CSG[:], func=AF.Exp, scale=1.0 / 16, bias=GSC[:, 0:1]), reads=["CSG", "GSC"], writes=["EKE"])
                S.add("act", lambda e: e.activation(out=GSC[:, 1:2], in_=GSC[:, 0:1], func=AF.Exp), reads=["GSC"], writes=["GSC"])
                S.add("dve", lambda e: e.scalar_tensor_tensor(out=QE[:], in0=GQK[:, 0:128], scalar=float(32 ** -0.5), in1=EQ[:], op0=ALU.mult, op1=ALU.mult), reads=["GQK", "EQ"], writes=["QE"])
                S.add("dve", lambda e: e.tensor_tensor(out=KE[:], in0=GQK[:, 128:256], in1=EK[:], op=ALU.mult), reads=["GQK", "EK"], writes=["KE"])
                S.add("dve", lambda e: e.tensor_tensor(out=KEF[:], in0=GQK[:, 128:256], in1=EKE[:], op=ALU.mult), reads=["GQK", "EKE"], writes=["KEF"])
                for h in range(4):
                    S.add("pool", lambda e, h=h: e.tensor_scalar(out=KEP[:, h, :], in0=KE[:], scalar1=CST[:, K_HM + h:K_HM + h + 1], scalar2=None, op0=ALU.mult), reads=["KE", "CST"], writes=["KEP"])
                pb, pbn = nb()
                pbb = pb[:, :].bitcast(BF16)
                S.add("pe", lambda e, pbb=pbb: e.transpose(pbb[:, 0:128], in_=KEF[:], identity=IDb[:]), reads=["KEF", "IDb"], writes=[pbn])
                S.add("act", lambda e, pbb=pbb: e.activation(out=KETM[:], in_=pbb[:, 0:128], func=AF.Copy), writes=[pbn, "KETM"])
                pb, pbn = nb()
                for h in range(4):
                    S.add("pe", lambda e, pb=pb, h=h: e.matmul(pb[:, h * 128:(h + 1) * 128], lhsT=KEP[:, h, :], rhs=QE[:], start=True, stop=True), reads=["KEP", "QE"], writes=[pbn])
                S.add("dve", lambda e, pb=pb: e.tensor_tensor(out=PTG[:], in0=pb[:, :], in1=cst(K_MI4, K_MI4 + 512), op=ALU.mult), reads=["CST"], writes=[pbn, "PTG"])
                for c in range(2):
                    pb, pbn = nb()
                    for hh in range(2):
                        h = 2 * c + hh
                        hs = 64 * hh
                        S.add("pe", lambda e, pb=pb, h=h, hs=hs: e.matmul(pb[hs:hs + 64, 0:128], lhsT=GVb[:, h * 64:(h + 1) * 64], rhs=PTG[:, h * 128:(h + 1) * 128], start=True, stop=False), reads=["GVb", "PTG"], writes=[pbn])
                        S.add("pe", lambda e, pb=pb, h=h, hs=hs: e.matmul(pb[hs:hs + 64, 0:128], lhsT=GSb[:, h * 64:(h + 1) * 64], rhs=QE[:], start=False, stop=True), reads=["GSb", "QE"], writes=[pbn])
                    S.add("dve", lambda e, pb=pb, c=c: e.tensor_copy(out=YY[:, c, :], in_=pb[:, 0:128]), writes=[pbn, "YY"])
                pb, pbn = nb()
                S.add("pe", lambda e, pb=pb: e.matmul(pb[:, 0:256], lhsT=KETM[:], rhs=GVb[:], start=True, stop=True), reads=["KETM", "GVb"], writes=[pbn])
                S.add("dve", lambda e, pb=pb: e.tensor_tensor(out=GTMP[:], in0=pb[:, 0:256], in1=cst(K_BDM, K_BDM + 256), op=ALU.mult), reads=["CST"], writes=[pbn, "GTMP"])
                S.add("dve", lambda e: e.scalar_tensor_tensor(out=GS[:], in0=GS[:], scalar=GSC[:, 1:2], in1=GTMP[:], op0=ALU.mult, op1=ALU.add), reads=["GS", "GSC", "GTMP"], writes=["GS"])
                S.add("pool", lambda e: e.tensor_copy(out=GSb[:], in_=GS[:]), reads=["GS"], writes=["GSb"])
                for c in range(2):
                    S.add("act", lambda e, c=c: e.activation(out=KK2[:, c, :], in_=YY[:, c, :], func=AF.Square), reads=["YY"], writes=["KK2"])
                    pb, pbn = nb()
                    S.add("pe", lambda e, pb=pb, c=c: e.matmul(pb[:, 0:128], lhsT=BD64S[:], rhs=KK2[:, c, :], start=True, stop=True), reads=["BD64S", "KK2"], writes=[pbn])
                    S.add("act", lambda e, pb=pb, c=c: e.activation(out=RST[:, c, :], in_=pb[:, 0:128], func=AF.Ln, bias=1e-5), writes=[pbn, "RST"])
                    S.add("act", lambda e, c=c: e.activation(out=RST[:, c, :], in_=RST[:, c, :], func=AF.Exp, scale=-0.5), reads=["RST"], writes=["RST"])
                    S.add("dve", lambda e, c=c: e.scalar_tensor_tensor(out=YY[:, c, :], in0=YY[:, c, :], scalar=PCOL[:, C_GNW + c:C_GNW + c + 1], in1=RST[:, c, :], op0=ALU.mult, op1=ALU.mult), reads=["YY", "PCOL", "RST"], writes=["YY"])
                    S.add("dve", lambda e, c=c: e.tensor_tensor(out=CAT[:, 6 + c, :], in0=YY[:, c, :], in1=SRGG[:, (2 + c) * 128:(3 + c) * 128], op=ALU.mult), reads=["YY", "SRGG"], writes=["CAT"])

                dump("CAT_%d_%d" % (l, t), CAT[:], ["CAT"], [128, 8, 128], BF16)
                dump("HT_%d_%d" % (l, t), ht[:], [htn], [128, 8, 130], BF16)
                dump("YY_%d_%d" % (l, t), YY[:], ["YY"], [128, 2, 128])
                dump("QT_%d_%d" % (l, t), QT[:], ["QT"], [128, 512], BF16)
                dump("RKV_%d_%d" % (l, t), RKV[:], ["RKV"], [128, 6, 128])
                dump("PC2_%d_%d" % (l, t), PC2[:], ["PC2"], [128, 48])
                dump("ELL_%d_%d" % (l, t), ELL[:], ["ELL"], [128, 2, 128])
                dump("AA_%d_%d" % (l, t), AA[:], ["AA"], [128, 2, 128])
                dump("BON_%d_%d" % (l, t), BON[:], ["BON"], [128, 2, 128])
                dump("LG_%d_%d" % (l, t), LG[:], ["LG"], [128, 128])
                dump("SRGG_%d_%d" % (l, t), SRGG[:], ["SRGG"], [128, 512], BF16)
                xo, xon = XO[p], "XO%d" % p
                pbs = []
                for half in range(2):
                    pb, pbn = nb()
                    pbs.append((pb, pbn))
                    for k in range(8):
                        S.add("pe", lambda e, pb=pb, k=k, half=half: e.matmul(pb[:, :], lhsT=CAT[:, k, :], rhs=WOUT[:, k, half * 512:(half + 1) * 512], start=(k == 0), stop=(k == 7)), reads=["CAT", "WOUT"], writes=[pbn])
                    S.add("act", lambda e, pb=pb, half=half: e.activation(out=XN[:, half * 512:(half + 1) * 512], in_=pb[:, :], func=AF.Square, accum_out=SC[:, 4 + half:5 + half]), reads=[], writes=[pbn, "XN", "SC"])
                S.add("dve", lambda e: e.tensor_tensor(out=SC[:, 6:7], in0=SC[:, 4:5], in1=SC[:, 5:6], op=ALU.add), reads=["SC"], writes=["SC"])
                S.add("act", lambda e: e.activation(out=SC[:, 7:8], in_=SC[:, 6:7], func=AF.Ln, scale=1.0 / D, bias=1e-6), reads=["SC"], writes=["SC"])
                S.add("act", lambda e: e.activation(out=SC[:, 8:9], in_=SC[:, 7:8], func=AF.Exp, scale=-0.5), reads=["SC"], writes=["SC"])
                for half in range(2):
                    pb, pbn = pbs[half]
                    S.add("dve", lambda e, pb=pb, half=half, xo=xo: e.scalar_tensor_tensor(out=xo[:, half * 512:(half + 1) * 512], in0=pb[:, :], scalar=SC[:, 8:9], in1=GG[:, half * 512:(half + 1) * 512], op0=ALU.mult, op1=ALU.mult),
                          reads=["SC", "GG"], writes=[pbn, xon])
                S.add("pool", lambda e, xo=xo, xt=xt: e.tensor_tensor(out=xo[:], in0=xo[:], in1=xt[:], op=ALU.add), reads=[xon, xtn], writes=[xon])
                S.add("sp", lambda e, xo=xo, t=t, dst=dst: e.dma_start(out=dst[t * 128:(t + 1) * 128, :], in_=xo[:]), reads=[xon], dma_key=("xs", p))

        S.barrier()
        with ExitStack() as es2:
            sems = {e: es2.enter_context(nc.semaphore("s_" + e)) for e in ENGINES}
            dsems = {k: es2.enter_context(nc.semaphore("d_%d" % i)) for i, k in enumerate(S.dma_tot)}
            block = es2.enter_context(nc.Block())
            S.emit(block, sems, dsems)
    return nc, dbg_out


def _consts():
    c = np.zeros((128, K_END), np.float32)
    p = np.arange(128)[:, None]
    f = np.arange(128)[None, :]
    c[:, K_ID:K_ID + 128] = np.eye(128, dtype=np.float32)
    c[:, K_MU2:K_MU2 + 128] = (p < f)
    c[:, K_MU2 + 128:K_MU2 + 256] = (p <= f)
    c[:, K_MLS:K_MLS + 128] = (p > f)
    c[:, K_BD64:K_BD64 + 128] = ((p // 64) == (f // 64))
    f2 = np.arange(256)[None, :]
    c[:, K_BDM:K_BDM + 256] = ((p // 32) == (f2 // 64))
    c[:, K_MI4:K_MI4 + 512] = np.tile((p <= f).astype(np.float32), (1, 4))
    for h in range(4):
        c[:, K_HM + h] = (np.arange(128) // 32 == h)
    qi = np.arange(128)[:, None]
    kj = np.arange(256)[None, :]
    dist = qi - kj + 128
    valid = (dist >= 0) & (dist < 128)
    dm = np.where(valid, dist.astype(np.float32), 1.0e9).astype(np.float32)
    c[:, K_AB:K_AB + 256] = dm
    dm0 = dm.copy()
    dm0[:, 0:128] = 1.0e9
    c[:, K_AB0:K_AB0 + 256] = dm0
    c[:, K_ONES:K_ONES + 128] = 1.0
    return c


def _pcol(inp, L):
    pc = np.zeros((L, 128, NCOL), np.float32)
    colv = lambda v, n: np.asarray(v, np.float32).reshape(n, 128).T
    for l in range(L):
        pc[l, :, C_GPRE:C_GPRE + 8] = colv(inp["norm_pre"][l], 8)
        pc[l, :, C_GPOST:C_GPOST + 8] = colv(inp["norm_post"][l], 8)
        pc[l, :, C_MUW:C_MUW + 8] = colv(inp["rwkv_mu_w"][l], 8)
        pc[l, :, C_MUA:C_MUA + 8] = colv(inp["rwkv_mu_a"][l], 8)
        pc[l, :, C_ADAB:C_ADAB + 24] = colv(inp["ada_b"][l], 24)
        pc[l, :, C_W0:C_W0 + 2] = colv(inp["rwkv_w0"][l], 2)
        pc[l, :, C_A0:C_A0 + 2] = colv(inp["rwkv_a0"][l], 2)
        pc[l, :, C_KK:C_KK + 2] = colv(inp["rwkv_k_k"][l], 2)
        pc[l, :, C_KA:C_KA + 2] = colv(inp["rwkv_k_a"][l], 2)
        pc[l, :, C_RK:C_RK + 2] = colv(np.asarray(inp["rwkv_r_k"][l]).reshape(256), 2)
        pc[l, :, C_LNW:C_LNW + 2] = colv(inp["rwkv_ln_w"][l], 2)
        pc[l, :, C_LNB:C_LNB + 2] = colv(inp["rwkv_ln_b"][l], 2)
        pc[l, :, C_GKB] = np.asarray(inp["gla_gk_b"][l], np.float32)
        nw = np.tile(np.asarray(inp["gla_norm_w"][l], np.float32), 2)
        pc[l, :, C_GNW] = nw
        pc[l, :, C_GNW + 1] = nw
        pc[l, :, C_SINK:C_SINK + 8] = np.asarray(inp["attn_sinks"][l], np.float32)[None, :]
        pc[l, :, C_C:C_C + 8] = colv(np.asarray(inp["c"]).reshape(D), 8)
    return pc


_CACHE = {}


def run(inputs, T, L, debug=None, ncores=8):
    inp = {k: np.asarray(v) for k, v in inputs.items()}
    key = (T, L, tuple(sorted(debug)) if debug else None)
    if key not in _CACHE:
        _CACHE[key] = build_program(T, L, debug)
    nc, dbg = _CACHE[key]
    f32 = lambda a: np.ascontiguousarray(np.asarray(a, np.float32))
    m = {
        "x": f32(inp["x"].reshape(-1, D)[:T]),
        "w_in": f32(inp["w_in"][:L]), "w_out": f32(inp["w_out"][:L]), "ada_w": f32(inp["ada_w"][:L]),
        "rwkv_w1": f32(inp["rwkv_w1"][:L]), "rwkv_a1": f32(inp["rwkv_a1"][:L]), "rwkv_w2": f32(inp["rwkv_w2"][:L]), "rwkv_a2": f32(inp["rwkv_a2"][:L]),
        "gla_gk1": f32(inp["gla_gk1"][:L]), "gla_gk2": f32(inp["gla_gk2"][:L]), "rwkv_mu_rkv": f32(inp["rwkv_mu_rkv"][:L]),
        "pcol": _pcol(inp, L), "cst": _consts(),
    }
    res = run_bass_kernel_spmd(nc, [m for _ in range(ncores)], core_ids=list(range(ncores)))
    r0 = res.results[0]
    out = np.asarray(r0["y"], np.float32).reshape(1, T, D)
    if debug:
        return out, {k: np.asarray(r0["dbg_" + k]).astype(np.float32) for k in dbg}
    return out


def kernel(**inputs):
    return run(inputs, 16384, 4)
```

```python
import os as _os
import numpy as np
from contextlib import ExitStack
import concourse.bass as bass
import concourse.mybir as mybir
from concourse.bass_utils import run_bass_kernel_spmd

F32 = mybir.dt.float32
BF16 = mybir.dt.bfloat16
AF = mybir.ActivationFunctionType
ALU = mybir.AluOpType
AX = mybir.AxisListType

ENGINES = ("pe", "act", "dve", "pool", "sp")
D = 1024
INW = 3072
NCOL = 89
C_GPRE, C_GPOST, C_MUW, C_MUA, C_ADAB = 0, 8, 16, 24, 32
C_W0, C_A0, C_KK, C_KA, C_RK, C_LNW, C_LNB, C_GKB, C_GNW, C_SINK, C_C = 56, 58, 60, 62, 64, 66, 68, 70, 71, 73, 81
K_ID, K_MU2, K_MLS, K_BD64, K_BDM, K_MI4, K_HM, K_AB, K_AB0, K_ONES, K_I2, K_END = 0, 128, 384, 512, 640, 896, 1408, 1412, 1668, 1924, 2052, 2116
NCC = 516
SLOPES = [float(2.0 ** (-8.0 * (h + 1) / 8)) for h in range(8)]
DECAY_K = float(np.exp(-0.5))


class Buf:
    __slots__ = ("name", "w", "r")

    def __init__(self, name):
        self.name = name
        self.w = None
        self.r = []


class Op:
    __slots__ = ("eng", "idx", "fn", "deps", "dma_key", "dma_cnt", "inc")

    def __init__(self, eng, idx, fn):
        self.eng = eng
        self.idx = idx
        self.fn = fn
        self.deps = []
        self.dma_key = None
        self.dma_cnt = 0
        self.inc = 16


class Sched:
    def __init__(self, nc):
        self.nc = nc
        self.ops = {e: [] for e in ENGINES}
        self.bufs = {}
        self.dma_tot = {}

    def buf(self, key):
        b = self.bufs.get(key)
        if b is None:
            b = Buf(key)
            self.bufs[key] = b
        return b

    def _bl(self, xs):
        return [x if isinstance(x, Buf) else self.buf(x) for x in xs]

    _defer = None

    def begin_defer(self):
        self._defer = []

    def end_defer(self):
        l = self._defer
        self._defer = None
        return l

    def merge(self, lists):
        idx = [0] * len(lists)
        while True:
            best, bi = None, -1
            for i, l in enumerate(lists):
                if idx[i] < len(l):
                    r = idx[i] / float(len(l))
                    if best is None or r < best:
                        best, bi = r, i
            if bi < 0:
                break
            self.add(*lists[bi][idx[bi]])
            idx[bi] += 1

    def add(self, eng, fn, reads=(), writes=(), dma_key=None, inc=16):
        if self._defer is not None:
            self._defer.append((eng, fn, reads, writes, dma_key, inc))
            return None
        reads = self._bl(reads)
        writes = self._bl(writes)
        op = Op(eng, len(self.ops[eng]), fn)
        deps = set()
        for b in reads:
            if b.w is not None:
                deps.add(b.w)
        for b in writes:
            if b.w is not None:
                deps.add(b.w)
            for r in b.r:
                deps.add(r)
        op.deps = list(deps)
        for b in reads:
            b.r.append(op)
        for b in writes:
            b.w = op
            b.r = []
        if dma_key is not None:
            op.dma_key = dma_key
            op.inc = inc
            self.dma_tot[dma_key] = self.dma_tot.get(dma_key, 0) + inc
            op.dma_cnt = self.dma_tot[dma_key]
        self.ops[eng].append(op)
        return op

    def nop(self, eng, deps):
        op = Op(eng, len(self.ops[eng]), None)
        op.deps = list(deps)
        self.ops[eng].append(op)
        return op

    def barrier(self):
        last = [self.ops[e][-1] for e in ENGINES if self.ops[e]]
        lastdma = {}
        for e in ENGINES:
            for o in self.ops[e]:
                if o.dma_key is not None:
                    lastdma[o.dma_key] = o
        deps = last + list(lastdma.values())
        for e in ENGINES:
            self.nop(e, deps)

    def emit(self, block, sems, dsems):
        ops = self.ops

        def run(ename, eng):
            waited = {}
            for op in ops[ename]:
                need = {}
                for d in op.deps:
                    if d.dma_key is not None:
                        k = ("d", d.dma_key)
                        v = d.dma_cnt
                    else:
                        if d.eng == ename and (ename == "pe" or d.idx >= op.idx):
                            continue
                        k = ("e", d.eng)
                        v = d.idx + 1
                    if v > need.get(k, 0):
                        need[k] = v
                for k, v in need.items():
                    if waited.get(k, 0) >= v:
                        continue
                    waited[k] = v
                    s = dsems[k[1]] if k[0] == "d" else sems[k[1]]
                    eng.wait_ge(s, v)
                if op.fn is None:
                    eng.sem_inc(sems[ename], 1)
                    continue
                ins = op.fn(eng)
                if op.dma_key is not None:
                    ins.then_inc(dsems[op.dma_key], op.inc)
                    eng.sem_inc(sems[ename], 1)
                else:
                    ins.then_inc(sems[ename], 1)

        @block.tensor
        def _(e):
            run("pe", e)

        @block.scalar
        def _(e):
            run("act", e)

        @block.vector
        def _(e):
            run("dve", e)

        @block.gpsimd
        def _(e):
            run("pool", e)

        @block.sync
        def _(e):
            run("sp", e)


def build_program(T, L, debug=None, NCORES=8):
    NT = T // 128
    nc = bass.Bass("TRN2", target_bir_lowering=False)
    dI = lambda name, shape: nc.dram_tensor(name, list(shape), F32, kind="ExternalInput").ap()
    x_d = dI("x", [T, D])
    xh_d = dI("xh", [128, D])
    flg_d = dI("flg", [128, 24])
    win_d = dI("w_in", [L, D, INW])
    wout_d = dI("w_out", [L, D, D])
    adaw_d = dI("ada_w", [L, D, 3 * D])
    w1_d = dI("rwkv_w1", [L, D, 64])
    a1_d = dI("rwkv_a1", [L, D, 64])
    w2_d = dI("rwkv_w2", [L, 64, 256])
    a2_d = dI("rwkv_a2", [L, 64, 256])
    gk1_d = dI("gla_gk1", [L, D, 16])
    gk2_d = dI("gla_gk2", [L, 16, 128])
    murkv_d = dI("rwkv_mu_rkv", [L, 768])
    pcol_d = dI("pcol", [L, 128, NCOL])
    cst_d = dI("cst", [128, K_END])
    y_d = nc.dram_tensor("y", [T, D], F32, kind="ExternalOutput").ap()
    xs_d = [nc.dram_tensor("xscr%d" % i, [T, D], F32).ap() for i in range(2)]
    spb_d = nc.dram_tensor("spillb", [NT, 128, 1408], BF16).ap()
    spf_d = nc.dram_tensor("spillf", [NT, 128, 768], F32).ap()
    ccin_t = nc.dram_tensor("cc_in", [128, NCC + D], F32)
    ccout_t = nc.dram_tensor("cc_out", [NCORES * 128, NCC + D], F32)
    dbg_out = {}

    S = Sched(nc)
    es = ExitStack()
    with es:
        def sb(name, shape, dt=F32):
            return es.enter_context(nc.sbuf_tensor(name, list(shape), dt))

        banks = [es.enter_context(nc.psum_tensor("pb%d" % i, [128, 512], F32)) for i in range(8)]
        bank_ctr = [0]
        bank_grp = [None]
        grp_ctr = {}

        def nb():
            g = bank_grp[0]
            if g is None:
                i = bank_ctr[0] % 8
                bank_ctr[0] += 1
            else:
                n = grp_ctr.get(g, 0)
                grp_ctr[g] = n + 1
                i = g[n % len(g)]
            return banks[i], "pb%d" % i

        CST = sb("CST", [128, K_END])
        IDb = sb("IDb", [128, 128], BF16)
        BD64S = sb("BD64S", [128, 128])
        PCOL = sb("PCOL", [128, NCOL])
        PC2 = sb("PC2", [128, 48])
        CACT = sb("CACT", [128, 8])
        WIN = sb("WIN", [128, 8, INW], BF16)
        WRB = sb("WRB", [128, 8, 768], BF16)
        WKD = sb("WKD", [128, 8, 256], BF16)
        WOUT = sb("WOUT", [128, 8, D], BF16)
        WL1A = sb("WL1A", [128, 8, 128], BF16)
        WL1B = sb("WL1B", [128, 8, 128], BF16)
        WG1 = sb("WG1", [128, 8, 16], BF16)
        W2P = sb("W2P", [128, 256], BF16)
        A2P = sb("A2P", [128, 256], BF16)
        GK2 = sb("GK2", [16, 128], BF16)
        STG = [sb("STG%d" % i, [128, 1536]) for i in range(2)]
        LST = STG[0][:, 0:1152].rearrange("p (k n) -> p k n", k=8)
        L2ST = STG[1]
        G2ST = sb("G2ST", [16, 128])
        GG = sb("GG", [128, D])
        DG = sb("DG", [128, 128])
        XT = [sb("XT%d" % i, [128, D]) for i in range(2)]
        XO = [sb("XO%d" % i, [128, D]) for i in range(2)]
        XN = sb("XN", [128, D])
        MUR = XN
        OMMUR = XO[0]
        SC = sb("SC", [128, 16])
        HT = [sb("HT%d" % i, [128, 8, 130], BF16) for i in range(2)]
        QT = sb("QT", [128, 512], BF16)
        KT = [sb("KT%d" % i, [128, 256], BF16) for i in range(2)]
        AV = [sb("AV%d" % i, [128, 128], BF16) for i in range(2)]
        SAG = sb("SAG", [128, 512], BF16)
        SPB = [sb("SPB%d" % i, [128, 1408], BF16) for i in range(2)]
        SPF = [sb("SPF%d" % i, [128, 768]) for i in range(2)]
        FLG = sb("FLG", [128, 24])
        SST = sb("SST", [128, 2, 64]); SSTb = sb("SSTb", [128, 2, 64], BF16)
        GST = sb("GST", [128, 256]); GSTb = sb("GSTb", [128, 256], BF16)
        RKV = sb("RKV", [128, 6, 128])
        GQK = sb("GQK", [128, 256])
        L1b = sb("L1b", [128, 128], BF16)
        G1b = sb("G1b", [16, 128], BF16)
        RVb = sb("RVb", [128, 256], BF16)
        GVb = sb("GVb", [128, 256], BF16)
        CAT = sb("CAT", [128, 4, 128], BF16)
        SS = [sb("SS%d" % i, [128, 256]) for i in range(2)]
        PP = [sb("PP%d" % i, [128, 256], BF16) for i in range(2)]
        DGb = [sb("DGb%d" % i, [128, 128], BF16) for i in range(2)]
        PTN = [sb("PTN%d" % i, [128, 256], BF16) for i in range(2)]
        ASC = [sb("ASC%d" % i, [128, 8]) for i in range(2)]
        ELL = sb("ELL", [128, 2, 128]); CS = sb("CS", [128, 2, 128]); AA = sb("AA", [128, 2, 128])
        EP = sb("EP", [128, 2, 128]); EI = sb("EI", [128, 2, 128]); EPV = sb("EPV", [128, 2, 128]); EE = sb("EE", [128, 2, 128])
        KKt = sb("KKt", [128, 2, 128]); KK2 = sb("KK2", [128, 2, 128]); KKN = sb("KKN", [128, 2, 128]); TKA = sb("TKA", [128, 2, 128])
        KP = sb("KP", [128, 2, 128]); TMP = sb("TMP", [128, 2, 128]); PRD = TKA
        AR = sb("AR", [128, 2, 256], BF16)
        BTIL = sb("BTIL", [128, 2, 128], BF16); KTIL = sb("KTIL", [128, 2, 128], BF16)
        BENDF = sb("BENDF", [128, 2, 128], BF16); KENDF = sb("KENDF", [128, 2, 128], BF16)
        BENDT = sb("BENDT", [128, 2, 128], BF16); KENDT = sb("KENDT", [128, 2, 128], BF16)
        RSC = sb("RSC", [128, 16])
        PQ = [[sb("PQ%d_%d" % (h, i), [128, 256], BF16) for i in range(2)] for h in range(4)]
        TTf = [sb("TTf%d" % h, [128, 128]) for h in range(4)]
        TTb = [sb("TTb%d" % h, [128, 128], BF16) for h in range(4)]
        AKB = [sb("AKB%d" % h, [128, 384], BF16) for h in range(4)]
        Wb = [sb("Wb%d" % h, [128, 128], BF16) for h in range(4)]
        Ub = [sb("Ub%d" % h, [128, 128], BF16) for h in range(4)]
        XS = sb("XS", [128, 2, 128]); XSb = sb("XSb", [128, 2, 128], BF16)
        YY = ELL; DD = CS; RST = AA
        LG = sb("LG", [128, 128]); CSG = sb("CSG", [128, 128]); GEX = sb("GEX", [128, 3, 128])
        QE = sb("QE", [128, 128], BF16); KE = sb("KE", [128, 128], BF16); KEP = sb("KEP", [128, 4, 128], BF16)
        KEF = sb("KEF", [128, 128], BF16); KETM = sb("KETM", [128, 128], BF16)
        PTG = sb("PTG", [128, 512], BF16)
        GS = sb("GS", [128, 256]); GSb = sb("GSb", [128, 256], BF16); GTMP = sb("GTMP", [128, 256])
        GSC = sb("GSC", [128, 8])
        ADT = sb("ADT", [128, 24])

        cst = lambda a, b: CST[:, a:b]
        IDf = cst(K_ID, K_ID + 128)
        ONESf = cst(K_ONES, K_ONES + 128)

        def dump(name, ap, reads, shape, dt=F32):
            if debug is None or name.split("_")[0] not in debug:
                return
            o = nc.dram_tensor("dbg_" + name, list(shape), dt, kind="ExternalOutput").ap()
            dbg_out[name] = o
            S.add("sp", lambda e: e.dma_start(out=o, in_=ap), reads=reads, dma_key="dbg")

        for a in range(0, K_END, 1024):
            b = min(K_END, a + 1024)
            S.add("sp", lambda e, a=a, b=b: e.dma_start(out=CST[:, a:b], in_=cst_d[:, a:b]), writes=["CST"], dma_key="init")
        S.add("sp", lambda e: e.dma_start(out=FLG[:], in_=flg_d[:, :]), writes=["FLG"], dma_key="init")
        S.barrier()
        S.add("dve", lambda e: e.tensor_copy(out=IDb[:], in_=IDf), reads=["CST"], writes=["IDb"])
        S.add("dve", lambda e: e.tensor_scalar(out=BD64S[:], in0=cst(K_BD64, K_BD64 + 128), scalar1=1.0 / 64, scalar2=None, op0=ALU.mult), reads=["CST"], writes=["BD64S"])
        for i in range(2):
            S.add("pool", lambda e, i=i: e.memset(HT[i][:], 0.0), writes=["HT%d" % i])
            S.add("pool", lambda e, i=i: e.memset(KT[i][:], 0.0), writes=["KT%d" % i])
            S.add("pool", lambda e, i=i: e.memset(AV[i][:], 0.0), writes=["AV%d" % i])
        S.add("pool", lambda e: e.memset(W2P[:], 0.0), writes=["W2P"])
        S.add("pool", lambda e: e.memset(A2P[:], 0.0), writes=["A2P"])

        stg_ctr = [0]

        for l in range(L):
            S.barrier()
            S.add("sp", lambda e, l=l: e.dma_start(out=PCOL[:], in_=pcol_d[l]), writes=["PCOL"], dma_key="pl")
            S.add("sp", lambda e, l=l: e.dma_start(out=MUR[:, 0:768], in_=murkv_d[l:l + 1, :].partition_broadcast(128)), writes=["XN"], dma_key="pl")
            S.add("sp", lambda e, l=l: e.dma_start(out=LST[:, :, 0:64], in_=w1_d[l].rearrange("(k p) n -> p k n", p=128)), writes=["STG0"], dma_key="pl")
            S.add("sp", lambda e, l=l: e.dma_start(out=LST[:, :, 64:128], in_=a1_d[l].rearrange("(k p) n -> p k n", p=128)), writes=["STG0"], dma_key="pl")
            S.add("sp", lambda e, l=l: e.dma_start(out=LST[:, :, 128:144], in_=gk1_d[l].rearrange("(k p) n -> p k n", p=128)), writes=["STG0"], dma_key="pl")
            S.add("sp", lambda e, l=l: e.dma_start(out=L2ST[0:64, 0:256], in_=w2_d[l]), writes=["STG1"], dma_key="pl")
            S.add("sp", lambda e, l=l: e.dma_start(out=L2ST[64:128, 0:256], in_=a2_d[l]), writes=["STG1"], dma_key="pl")
            S.add("sp", lambda e, l=l: e.dma_start(out=G2ST[:], in_=gk2_d[l]), writes=["G2ST"], dma_key="pl")
            S.barrier()
            S.add("dve", lambda e: e.tensor_scalar(out=OMMUR[:, 0:768], in0=MUR[:, 0:768], scalar1=-1.0, scalar2=1.0, op0=ALU.mult, op1=ALU.add), reads=["XN"], writes=["XO0"])
            S.add("pool", lambda e: e.tensor_copy(out=W2P[0:64, :], in_=L2ST[0:64, 0:256]), reads=["STG1"], writes=["W2P"])
            S.add("pool", lambda e: e.tensor_copy(out=A2P[64:128, :], in_=L2ST[64:128, 0:256]), reads=["STG1"], writes=["A2P"])
            S.add("pool", lambda e: e.tensor_copy(out=GK2[:], in_=G2ST[:]), reads=["G2ST"], writes=["GK2"])
            S.add("pool", lambda e: e.tensor_copy(out=WG1[:], in_=LST[:, :, 128:144]), reads=["STG0"], writes=["WG1"])
            S.add("dve", lambda e: e.tensor_scalar(out=SC[:, 0:16], in0=PCOL[:, C_MUW:C_MUW + 16], scalar1=-1.0, scalar2=1.0, op0=ALU.mult, op1=ALU.add), reads=["PCOL"], writes=["SC"])
            for k in range(8):
                S.add("dve", lambda e, k=k: e.tensor_scalar(out=WL1A[:, k, 0:64], in0=LST[:, k, 0:64], scalar1=SC[:, k:k + 1], scalar2=None, op0=ALU.mult), reads=["STG0", "SC"], writes=["WL1A"])
                S.add("dve", lambda e, k=k: e.tensor_scalar(out=WL1A[:, k, 64:128], in0=LST[:, k, 64:128], scalar1=SC[:, 8 + k:9 + k], scalar2=None, op0=ALU.mult), reads=["STG0", "SC"], writes=["WL1A"])
                S.add("dve", lambda e, k=k: e.tensor_scalar(out=WL1B[:, k, 0:64], in0=LST[:, k, 0:64], scalar1=PCOL[:, C_MUW + k:C_MUW + k + 1], scalar2=None, op0=ALU.mult), reads=["STG0", "PCOL"], writes=["WL1B"])
                S.add("dve", lambda e, k=k: e.tensor_scalar(out=WL1B[:, k, 64:128], in0=LST[:, k, 64:128], scalar1=PCOL[:, C_MUA + k:C_MUA + k + 1], scalar2=None, op0=ALU.mult), reads=["STG0", "PCOL"], writes=["WL1B"])
            S.add("act", lambda e: e.activation(out=CACT[:], in_=PCOL[:, C_C:C_C + 8], func=AF.Silu), reads=["PCOL"], writes=["CACT"])
            S.add("dve", lambda e: e.tensor_copy(out=ADT[:], in_=PCOL[:, C_ADAB:C_ADAB + 24]), reads=["PCOL"], writes=["ADT"])
            for k in range(8):
                pb, pbn = nb()
                for hf in range(2):
                    i = stg_ctr[0] % 2
                    stg_ctr[0] += 1
                    st = STG[i]
                    sn = "STG%d" % i
                    S.add("sp", lambda e, st=st, l=l, k=k, hf=hf: e.dma_start(out=st[:, :], in_=adaw_d[l, k * 128:(k + 1) * 128, hf * 1536:(hf + 1) * 1536]), writes=[sn], dma_key=("stg", i))
                    for jj in range(12):
                        j = hf * 12 + jj
                        S.add("pe", lambda e, st=st, k=k, pb=pb, j=j, jj=jj: e.matmul(pb[:, j:j + 1], lhsT=st[:, jj * 128:(jj + 1) * 128], rhs=CACT[:, k:k + 1], start=True, stop=True),
                              reads=[sn, "CACT"], writes=[pbn])
                S.add("dve", lambda e, pb=pb: e.tensor_tensor(out=ADT[:], in0=pb[:, 0:24], in1=ADT[:], op=ALU.add), reads=["ADT"], writes=[pbn, "ADT"])
            S.add("dve", lambda e: e.tensor_copy(out=PC2[:, 0:24], in_=ADT[:]), reads=["ADT"], writes=["PC2"])
            S.add("dve", lambda e: e.scalar_tensor_tensor(out=PC2[:, 24:32], in0=PC2[:, 8:16], scalar=1.0, in1=PCOL[:, C_GPRE:C_GPRE + 8], op0=ALU.add, op1=ALU.mult), reads=["PC2", "PCOL"], writes=["PC2"])
            S.add("dve", lambda e: e.tensor_scalar(out=PC2[:, 32:33], in0=PCOL[:, C_GKB:C_GKB + 1], scalar1=-1.0, scalar2=None, op0=ALU.mult), reads=["PCOL"], writes=["PC2"])
            S.add("dve", lambda e: e.tensor_tensor(out=PC2[:, 33:41], in0=PC2[:, 16:24], in1=PCOL[:, C_GPOST:C_GPOST + 8], op=ALU.mult), reads=["PC2", "PCOL"], writes=["PC2"])
            for half in range(2):
                pb, pbn = nb()
                for cc in range(4):
                    c = half * 4 + cc
                    S.add("dve", lambda e, c=c: e.tensor_scalar(out=DG[:], in0=IDf, scalar1=PC2[:, 33 + c:34 + c], scalar2=None, op0=ALU.mult), reads=["CST", "PC2"], writes=["DG"])
                    S.add("pe", lambda e, pb=pb, cc=cc: e.matmul(pb[:, cc * 128:(cc + 1) * 128], lhsT=ONESf, rhs=DG[:], start=True, stop=True), reads=["CST", "DG"], writes=[pbn])
                S.add("act", lambda e, pb=pb, half=half: e.activation(out=GG[:, half * 512:(half + 1) * 512], in_=pb[:, :], func=AF.Copy), reads=[], writes=[pbn, "GG"])
            for k in range(8):
                for hf in range(2):
                    i = stg_ctr[0] % 2
                    stg_ctr[0] += 1
                    st = STG[i]
                    sn = "STG%d" % i
                    S.add("sp", lambda e, st=st, l=l, k=k, hf=hf: e.dma_start(out=st[:, :], in_=win_d[l, k * 128:(k + 1) * 128, hf * 1536:(hf + 1) * 1536]), writes=[sn], dma_key=("stg", i))
                    if hf == 0:
                        S.add("act", lambda e, st=st, k=k: e.activation(out=WIN[:, k, 0:1280], in_=st[:, 0:1280], func=AF.Copy), reads=[sn], writes=["WIN"])
                        S.add("dve", lambda e, st=st, k=k: e.tensor_tensor(out=WIN[:, k, 1280:1536], in0=st[:, 1280:1536], in1=OMMUR[:, 0:256], op=ALU.mult), reads=[sn, "XO0"], writes=["WIN"])
                        S.add("dve", lambda e, st=st, k=k: e.tensor_tensor(out=WRB[:, k, 0:256], in0=st[:, 1280:1536], in1=MUR[:, 0:256], op=ALU.mult), reads=[sn, "XN"], writes=["WRB"])
                        for g in range(2):
                            for dup in range(2):
                                S.add("pool", lambda e, st=st, k=k, g=g, dup=dup: e.tensor_copy(out=WKD[:, k, g * 128 + dup * 64:g * 128 + dup * 64 + 64], in_=st[:, 512 + g * 64:512 + g * 64 + 64]),
                                      reads=[sn], writes=["WKD"])
                    else:
                        S.add("dve", lambda e, st=st, k=k: e.tensor_tensor(out=WIN[:, k, 1536:2048], in0=st[:, 0:512], in1=OMMUR[:, 256:768], op=ALU.mult), reads=[sn, "XO0"], writes=["WIN"])
                        S.add("dve", lambda e, st=st, k=k: e.tensor_tensor(out=WRB[:, k, 256:768], in0=st[:, 0:512], in1=MUR[:, 256:768], op=ALU.mult), reads=[sn, "XN"], writes=["WRB"])
                        S.add("act", lambda e, st=st, k=k: e.activation(out=WIN[:, k, 2048:3072], in_=st[:, 512:1536], func=AF.Copy), reads=[sn], writes=["WIN"])
            for k in range(8):
                i = stg_ctr[0] % 2
                stg_ctr[0] += 1
                st = STG[i]
                sn = "STG%d" % i
                S.add("sp", lambda e, st=st, l=l, k=k: e.dma_start(out=st[:, 0:D], in_=wout_d[l, k * 128:(k + 1) * 128, :]), writes=[sn], dma_key=("stg", i))
                S.add("act", lambda e, st=st, k=k: e.activation(out=WOUT[:, k, :], in_=st[:, 0:D], func=AF.Copy), reads=[sn], writes=["WOUT"])
            S.add("pool", lambda e: e.memset(XS[:], 0.0), writes=["XS"])
            for c in range(2):
                S.add("pool", lambda e, c=c: e.tensor_copy(out=XS[:, c, 64:128], in_=cst(K_I2, K_I2 + 64)), reads=["CST"], writes=["XS"])
            S.add("pool", lambda e: e.tensor_copy(out=XSb[:], in_=XS[:]), reads=["XS"], writes=["XSb"])
            S.add("pool", lambda e: e.memset(GS[:], 0.0), writes=["GS"])
            S.add("pool", lambda e: e.memset(GSb[:], 0.0), writes=["GSb"])
            S.add("pool", lambda e: e.memset(GSC[:, 2:3], 1.0), writes=["GSC"])
            S.barrier()

            src = x_d if l == 0 else xs_d[(l - 1) % 2]
            dst = y_d if l == L - 1 else xs_d[l % 2]

            def load_x(t, src=src):
                p = t % 2
                S.add("sp", lambda e: e.dma_start(out=XT[p][:], in_=src[t * 128:(t + 1) * 128, :]), writes=["XT%d" % p], dma_key=("xl", p))

            if l == 0:
                S.add("sp", lambda e: e.dma_start(out=XT[1][:], in_=xh_d[:, :]), writes=["XT1"], dma_key=("xl", 1))
            else:
                for j in range(NCORES):
                    i = stg_ctr[0] % 2
                    stg_ctr[0] += 1
                    st = STG[i]
                    sn = "STG%d" % i
                    S.add("sp", lambda e, st=st, j=j: e.dma_start(out=st[:, 0:D], in_=ccout_t.ap()[j * 128:(j + 1) * 128, NCC:NCC + D]), reads=["cc_out"], writes=[sn], dma_key=("stg", i))
                    if j == 0:
                        S.add("dve", lambda e, st=st, j=j: e.tensor_scalar(out=XT[1][:], in0=st[:, 0:D], scalar1=FLG[:, 8 + j:9 + j], scalar2=None, op0=ALU.mult), reads=[sn, "FLG"], writes=["XT1"])
                    else:
                        S.add("dve", lambda e, st=st, j=j: e.scalar_tensor_tensor(out=XT[1][:], in0=st[:, 0:D], scalar=FLG[:, 8 + j:9 + j], in1=XT[1][:], op0=ALU.mult, op1=ALU.add), reads=[sn, "FLG", "XT1"], writes=["XT1"])
            load_x(0)
            for t in [-1] + list(range(NT)):
                p = t % 2
                q = 1 - p
                if t >= 0 and t + 1 < NT:
                    load_x(t + 1)
                spb, spbn = SPB[p], "SPB%d" % p
                spf, spfn = SPF[p], "SPF%d" % p
                xt, xtn = XT[p], "XT%d" % p
                ht, htn = HT[p], "HT%d" % p
                S.add("act", lambda e, xt=xt: e.activation(out=XN[:], in_=xt[:], func=AF.Square, accum_out=SC[:, 0:1]), reads=[xtn], writes=["XN", "SC"])
                S.add("act", lambda e: e.activation(out=SC[:, 1:2], in_=SC[:, 0:1], func=AF.Ln, scale=1.0 / D, bias=1e-6), reads=["SC"], writes=["SC"])
                S.add("act", lambda e: e.activation(out=SC[:, 2:3], in_=SC[:, 1:2], func=AF.Exp, scale=-0.5), reads=["SC"], writes=["SC"])
                S.add("dve", lambda e, xt=xt: e.tensor_scalar(out=XN[:], in0=xt[:], scalar1=SC[:, 2:3], scalar2=None, op0=ALU.mult), reads=[xtn, "SC"], writes=["XN"])
                if t == -1:
                    S.add("pool", lambda e, ht=ht: e.memset(ht[:, :, 0:1], 0.0), writes=[htn])
                elif t == 0:
                    S.add("pool", lambda e, ht=ht, q=q: e.tensor_scalar(out=ht[:, :, 0:1], in0=HT[q][:, :, 128:129], scalar1=FLG[:, 16:17], scalar2=None, op0=ALU.mult), reads=["HT%d" % q, "FLG"], writes=[htn])
                else:
                    S.add("pool", lambda e, ht=ht, q=q: e.tensor_copy(out=ht[:, :, 0:1], in_=HT[q][:, :, 128:129]), reads=["HT%d" % q], writes=[htn])
                for half in range(2):
                    pb, pbn = nb()
                    for cc in range(4):
                        c = half * 4 + cc
                        S.add("pe", lambda e, pb=pb, cc=cc, c=c: e.transpose(pb[:, cc * 128:(cc + 1) * 128], in_=XN[:, c * 128:(c + 1) * 128], identity=IDf), reads=["XN", "CST"], writes=[pbn])
                    for cc in range(4):
                        c = half * 4 + cc
                        S.add("act", lambda e, pb=pb, cc=cc, c=c, ht=ht: e.activation(out=ht[:, c, 1:129], in_=pb[:, cc * 128:(cc + 1) * 128], func=AF.Identity,
                                                                                 scale=PC2[:, 24 + c:25 + c], bias=PC2[:, c:c + 1]), reads=["PC2"], writes=[pbn, htn])
                hc = lambda k, ht=ht: ht[:, k, 1:129]
                hp = lambda k, ht=ht: ht[:, k, 0:128]

                def proj_fm(out_ap, pbn, wfn, M, shift_wfn=None):
                    n = 16 if shift_wfn is not None else 8
                    i = 0
                    for k in range(8):
                        S.add("pe", lambda e, k=k, i=i, hk=hc(k): e.matmul(out_ap, lhsT=wfn(k), rhs=hk, start=(i == 0), stop=(i == n - 1)), reads=[htn, "WIN", "WRB", "WKD", "WL1A", "WL1B", "WG1"], writes=[pbn])
                        i += 1
                        if shift_wfn is not None:
                            S.add("pe", lambda e, k=k, i=i, hk=hp(k): e.matmul(out_ap, lhsT=shift_wfn(k), rhs=hk, start=False, stop=(i == n - 1)), reads=[htn, "WIN", "WRB", "WL1B"], writes=[pbn])
                            i += 1

                if t == -1:
                    pb, pbn = nb()
                    for g in range(2):
                        proj_fm(pb[:, g * 128:(g + 1) * 128], pbn, lambda k, g=g: WKD[:, k, g * 128:(g + 1) * 128], 128)
                    S.add("act", lambda e, pb=pb, p=p: e.activation(out=KT[p][:], in_=pb[:, 0:256], func=AF.Copy), writes=[pbn, "KT%d" % p])
                    pb, pbn = nb()
                    for k in range(8):
                        S.add("pe", lambda e, k=k, pb=pb, hk=hc(k): e.matmul(pb[:, 0:128], lhsT=hk, rhs=WIN[:, k, 640:768], start=(k == 0), stop=(k == 7)), reads=[htn, "WIN"], writes=[pbn])
                    S.add("act", lambda e, pb=pb, p=p: e.activation(out=AV[p][:], in_=pb[:, 0:128], func=AF.Copy), writes=[pbn, "AV%d" % p])
                    continue
                pb, pbn = nb()
                for c in range(4):
                    proj_fm(pb[:, c * 128:(c + 1) * 128], pbn, lambda k, c=c: WIN[:, k, c * 128:(c + 1) * 128], 128)
                S.add("act", lambda e, pb=pb: e.activation(out=QT[:], in_=pb[:, :], func=AF.Copy, scale=0.125), writes=[pbn, "QT"])
                pb, pbn = nb()
                for g in range(2):
                    proj_fm(pb[:, g * 128:(g + 1) * 128], pbn, lambda k, g=g: WKD[:, k, g * 128:(g + 1) * 128], 128)
                S.add("act", lambda e, pb=pb, p=p: e.activation(out=KT[p][:], in_=pb[:, 0:256], func=AF.Copy), writes=[pbn, "KT%d" % p])
                pb, pbn = nb()
                for c in range(4):
                    proj_fm(pb[:, c * 128:(c + 1) * 128], pbn, lambda k, c=c: WIN[:, k, 768 + c * 128:768 + (c + 1) * 128], 128)
                S.add("act", lambda e, pb=pb: e.activation(out=SAG[:], in_=pb[:, :], func=AF.Silu), writes=[pbn, "SAG"])
                pb, pbn = nb()
                for c in range(4):
                    proj_fm(pb[:, c * 128:(c + 1) * 128], pbn, lambda k, c=c: WIN[:, k, 1280 + c * 128:1280 + (c + 1) * 128], 128, lambda k, c=c: WRB[:, k, c * 128:(c + 1) * 128])
                S.add("dve", lambda e, pb=pb: e.tensor_copy(out=RKV[:, 0:4, :], in_=pb[:, :].rearrange("p (c n) -> p c n", c=4)), writes=[pbn, "RKV"])
                pb, pbn = nb()
                for c in range(4, 6):
                    proj_fm(pb[:, (c - 4) * 128:(c - 3) * 128], pbn, lambda k, c=c: WIN[:, k, 1280 + c * 128:1280 + (c + 1) * 128], 128, lambda k, c=c: WRB[:, k, c * 128:(c + 1) * 128])
                for j in range(2):
                    proj_fm(pb[:, (2 + j) * 128:(3 + j) * 128], pbn, lambda k, j=j: WIN[:, k, 2304 + j * 128:2304 + (j + 1) * 128], 128)
                S.add("dve", lambda e, pb=pb: e.tensor_copy(out=RKV[:, 4:6, :], in_=pb[:, 0:256].rearrange("p (c n) -> p c n", c=2)), writes=[pbn, "RKV"])
                S.add("act", lambda e, pb=pb: e.activation(out=GQK[:], in_=pb[:, 256:512], func=AF.Copy), writes=[pbn, "GQK"])
                pb, pbn = nb()
                for j, off in enumerate((2048, 2176, 2816, 2944)):
                    proj_fm(pb[:, j * 128:(j + 1) * 128], pbn, lambda k, off=off: WIN[:, k, off:off + 128], 128)
                S.add("act", lambda e, pb=pb, spb=spb: e.activation(out=spb[:, 512:1024], in_=pb[:, :], func=AF.Silu), writes=[pbn, spbn])
                pb, pbn = nb()
                proj_fm(pb[:, 0:128], pbn, lambda k: WL1A[:, k, :], 128, lambda k: WL1B[:, k, :])
                proj_fm(pb[0:16, 128:256], pbn, lambda k: WG1[:, k, :], 16)
                S.add("act", lambda e, pb=pb: e.activation(out=L1b[0:64, :], in_=pb[0:64, 0:128], func=AF.Tanh), writes=[pbn, "L1b"])
                S.add("act", lambda e, pb=pb: e.activation(out=L1b[64:128, :], in_=pb[64:128, 0:128], func=AF.Copy), writes=[pbn, "L1b"])
                S.add("act", lambda e, pb=pb: e.activation(out=G1b[:], in_=pb[0:16, 128:256], func=AF.Copy), writes=[pbn, "G1b"])
                pb, pbn = nb()
                for k in range(8):
                    S.add("pe", lambda e, k=k, pb=pb, hk=hc(k): e.matmul(pb[:, 0:128], lhsT=hk, rhs=WIN[:, k, 640:768], start=(k == 0), stop=(k == 7)), reads=[htn, "WIN"], writes=[pbn])
                i = 0
                for k in range(8):
                    S.add("pe", lambda e, k=k, pb=pb, i=i, hk=hc(k): e.matmul(pb[:, 128:384], lhsT=hk, rhs=WIN[:, k, 1792:2048], start=(i == 0), stop=False), reads=[htn, "WIN"], writes=[pbn])
                    i += 1
                    S.add("pe", lambda e, k=k, pb=pb, i=i, hk=hp(k): e.matmul(pb[:, 128:384], lhsT=hk, rhs=WRB[:, k, 512:768], start=False, stop=(i == 15)), reads=[htn, "WRB"], writes=[pbn])
                    i += 1
                S.add("act", lambda e, pb=pb, p=p: e.activation(out=AV[p][:], in_=pb[:, 0:128], func=AF.Copy), writes=[pbn, "AV%d" % p])
                S.add("dve", lambda e, pb=pb: e.tensor_copy(out=RVb[:], in_=pb[:, 128:384]), writes=[pbn, "RVb"])
                pb, pbn = nb()
                for k in range(8):
                    S.add("pe", lambda e, k=k, pb=pb, hk=hc(k): e.matmul(pb[:, 0:256], lhsT=hk, rhs=WIN[:, k, 2560:2816], start=(k == 0), stop=(k == 7)), reads=[htn, "WIN"], writes=[pbn])
                S.add("act", lambda e, pb=pb: e.activation(out=GVb[:], in_=pb[:, 0:256], func=AF.Copy), writes=[pbn, "GVb"])
                pb, pbn = nb()
                for c in range(2):
                    S.add("pe", lambda e, c=c, pb=pb: e.matmul(pb[:, c * 128:(c + 1) * 128], lhsT=W2P[:, c * 128:(c + 1) * 128], rhs=L1b[:], start=True, stop=True), reads=["W2P", "L1b"], writes=[pbn])
                    S.add("pe", lambda e, c=c, pb=pb: e.matmul(pb[:, (2 + c) * 128:(3 + c) * 128], lhsT=A2P[:, c * 128:(c + 1) * 128], rhs=L1b[:], start=True, stop=True), reads=["A2P", "L1b"], writes=[pbn])
                for c in range(2):
                    S.add("act", lambda e, c=c, pb=pb: e.activation(out=ELL[:, c, :], in_=pb[:, c * 128:(c + 1) * 128], func=AF.Sigmoid, bias=PCOL[:, C_W0 + c:C_W0 + c + 1]), reads=["PCOL"], writes=[pbn, "ELL"])
                    S.add("act", lambda e, c=c, pb=pb: e.activation(out=AA[:, c, :], in_=pb[:, (2 + c) * 128:(3 + c) * 128], func=AF.Sigmoid, bias=PCOL[:, C_A0 + c:C_A0 + c + 1]), reads=["PCOL"], writes=[pbn, "AA"])
                pb, pbn = nb()
                S.add("pe", lambda e, pb=pb: e.matmul(pb[:, 0:128], lhsT=GK2[:], rhs=G1b[:], start=True, stop=True), reads=["GK2", "G1b"], writes=[pbn])
                S.add("act", lambda e, pb=pb: e.activation(out=LG[:], in_=pb[:, 0:128], func=AF.Exp, scale=-1.0, bias=PC2[:, 32:33]), reads=["PC2"], writes=[pbn, "LG"])
                S.add("act", lambda e: e.activation(out=LG[:], in_=LG[:], func=AF.Ln, bias=1.0), reads=["LG"], writes=["LG"])

                S.begin_defer()
                bank_grp[0] = (4, 5)
                abk = K_AB0 if t == 0 else K_AB
                for c in range(4):
                    for hh in range(2):
                        h = 2 * c + hh
                        g = h // 4
                        hs = 64 * hh
                        a = h % 2
                        ss, pp, dgb, ptn, asc = SS[a], PP[a], DGb[a], PTN[a], ASC[a]
                        ssn, ppn, dgn, ptnn, ascn = "SS%d" % a, "PP%d" % a, "DGb%d" % a, "PTN%d" % a, "ASC%d" % a
                        pb, pbn = nb()
                        S.add("pe", lambda e, pb=pb, c=c, hs=hs, g=g, q=q: e.matmul(pb[:, 0:128], lhsT=QT[hs:hs + 64, c * 128:(c + 1) * 128], rhs=KT[q][hs:hs + 64, g * 128:(g + 1) * 128], start=True, stop=True),
                              reads=["QT", "KT%d" % q], writes=[pbn])
                        S.add("pe", lambda e, pb=pb, c=c, hs=hs, g=g, p=p: e.matmul(pb[:, 128:256], lhsT=QT[hs:hs + 64, c * 128:(c + 1) * 128], rhs=KT[p][hs:hs + 64, g * 128:(g + 1) * 128], start=True, stop=True),
                              reads=["QT", "KT%d" % p], writes=[pbn])
                        S.add("dve", lambda e, pb=pb, ss=ss, h=h, abk=abk: e.scalar_tensor_tensor(out=ss[:], in0=CST[:, abk:abk + 256], scalar=-SLOPES[h], in1=pb[:, 0:256], op0=ALU.mult, op1=ALU.add), reads=["CST"], writes=[pbn, ssn])
                        S.add("dve", lambda e, ss=ss, asc=asc: e.reduce_max(out=asc[:, 0:1], in_=ss[:], axis=AX.X), reads=[ssn], writes=[ascn])
                        S.add("dve", lambda e, asc=asc: e.tensor_scalar(out=asc[:, 1:2], in0=asc[:, 0:1], scalar1=-1.0, scalar2=None, op0=ALU.mult), reads=[ascn], writes=[ascn])
                        S.add("act", lambda e, ss=ss, pp=pp, asc=asc: e.activation(out=pp[:], in_=ss[:], func=AF.Exp, bias=asc[:, 1:2], accum_out=asc[:, 2:3]), reads=[ssn, ascn], writes=[ppn, ascn])
                        S.add("act", lambda e, asc=asc, h=h: e.activation(out=asc[:, 3:4], in_=asc[:, 1:2], func=AF.Exp, bias=PCOL[:, C_SINK + h:C_SINK + h + 1]), reads=[ascn, "PCOL"], writes=[ascn])
                        S.add("dve", lambda e, asc=asc: e.tensor_tensor(out=asc[:, 4:5], in0=asc[:, 2:3], in1=asc[:, 3:4], op=ALU.add), reads=[ascn], writes=[ascn])
                        S.add("dve", lambda e, asc=asc: e.reciprocal(out=asc[:, 5:6], in_=asc[:, 4:5]), reads=[ascn], writes=[ascn])
                        S.add("dve", lambda e, asc=asc, dgb=dgb: e.tensor_scalar(out=dgb[:], in0=IDf, scalar1=asc[:, 5:6], scalar2=None, op0=ALU.mult), reads=[ascn, "CST"], writes=[dgn])
                        pb2, pbn2 = nb()
                        for j in range(2):
                            S.add("pe", lambda e, pb2=pb2, pp=pp, dgb=dgb, j=j: e.matmul(pb2[:, j * 128:(j + 1) * 128], lhsT=pp[:, j * 128:(j + 1) * 128], rhs=dgb[:], start=True, stop=True), reads=[ppn, dgn], writes=[pbn2])
                        S.add("act", lambda e, pb2=pb2, ptn=ptn: e.activation(out=ptn[:], in_=pb2[:, 0:256], func=AF.Copy), writes=[pbn2, ptnn])
                    pb, pbn = nb()
                    for hh in range(2):
                        h = 2 * c + hh
                        g = h // 4
                        hs = 64 * hh
                        a = h % 2
                        S.add("pe", lambda e, pb=pb, hs=hs, g=g, a=a, q=q: e.matmul(pb[hs:hs + 64, 0:128], lhsT=AV[q][:, g * 64:(g + 1) * 64], rhs=PTN[a][:, 0:128], start=True, stop=False), reads=["AV%d" % q, "PTN%d" % a], writes=[pbn])
                        S.add("pe", lambda e, pb=pb, hs=hs, g=g, a=a, p=p: e.matmul(pb[hs:hs + 64, 0:128], lhsT=AV[p][:, g * 64:(g + 1) * 64], rhs=PTN[a][:, 128:256], start=False, stop=True), reads=["AV%d" % p, "PTN%d" % a], writes=[pbn])
                    S.add("dve", lambda e, pb=pb, c=c, spb=spb: e.tensor_tensor(out=spb[:, c * 128:(c + 1) * 128], in0=pb[:, 0:128], in1=SAG[:, c * 128:(c + 1) * 128], op=ALU.mult), reads=["SAG"], writes=[pbn, spbn])

                st_attn = S.end_defer()
                S.begin_defer()
                bank_grp[0] = (0, 1, 2, 3)
                R = lambda c: RKV[:, c, :]
                Kr = lambda c: RKV[:, 2 + c, :]
                Vr = lambda c: RKV[:, 4 + c, :]
                for c in range(2):
                    col = lambda base, c=c: PCOL[:, base + c:base + c + 1]
                    S.add("dve", lambda e, c=c: e.tensor_tensor_scan(out=CS[:, c, :], data0=ONESf, data1=ELL[:, c, :], initial=0.0, op0=ALU.mult, op1=ALU.add), reads=["CST", "ELL"], writes=["CS"])
                    S.add("act", lambda e, c=c: e.activation(out=EP[:, c, :], in_=CS[:, c, :], func=AF.Exp, scale=-DECAY_K), reads=["CS"], writes=["EP"])
                    S.add("act", lambda e, c=c: e.activation(out=EI[:, c, :], in_=CS[:, c, :], func=AF.Exp, scale=DECAY_K), reads=["CS"], writes=["EI"])
                    S.add("dve", lambda e, c=c: e.tensor_tensor(out=TMP[:, c, :], in0=CS[:, c, :], in1=ELL[:, c, :], op=ALU.subtract), reads=["CS", "ELL"], writes=["TMP"])
                    S.add("act", lambda e, c=c: e.activation(out=EPV[:, c, :], in_=TMP[:, c, :], func=AF.Exp, scale=-DECAY_K), reads=["TMP"], writes=["EPV"])
                    S.add("dve", lambda e, c=c: e.tensor_scalar(out=RSC[:, c:c + 1], in0=CS[:, c, 127:128], scalar1=-DECAY_K, scalar2=None, op0=ALU.mult), reads=["CS"], writes=["RSC"])
                    S.add("act", lambda e, c=c: e.activation(out=EE[:, c, :], in_=CS[:, c, :], func=AF.Exp, scale=DECAY_K, bias=RSC[:, c:c + 1]), reads=["CS", "RSC"], writes=["EE"])
                    S.add("act", lambda e, c=c: e.activation(out=RSC[:, 2 + c:3 + c], in_=RSC[:, c:c + 1], func=AF.Exp), reads=["RSC"], writes=["RSC"])
                    S.add("dve", lambda e, c=c, cc_=col(C_KK): e.tensor_scalar(out=KKt[:, c, :], in0=Kr(c), scalar1=cc_, scalar2=None, op0=ALU.mult), reads=["RKV", "PCOL"], writes=["KKt"])
                    S.add("act", lambda e, c=c: e.activation(out=KK2[:, c, :], in_=KKt[:, c, :], func=AF.Square), reads=["KKt"], writes=["KK2"])
                    pb, pbn = nb()
                    S.add("pe", lambda e, c=c, pb=pb: e.matmul(pb[:, 0:128], lhsT=cst(K_BD64, K_BD64 + 128), rhs=KK2[:, c, :], start=True, stop=True), reads=["CST", "KK2"], writes=[pbn])
                    S.add("dve", lambda e, c=c, pb=pb: e.tensor_scalar(out=TMP[:, c, :], in0=pb[:, 0:128], scalar1=1e-18, scalar2=None, op0=ALU.max), reads=[], writes=[pbn, "TMP"])
                    S.add("act", lambda e, c=c: e.activation(out=TMP[:, c, :], in_=TMP[:, c, :], func=AF.Ln), reads=["TMP"], writes=["TMP"])
                    S.add("act", lambda e, c=c: e.activation(out=TMP[:, c, :], in_=TMP[:, c, :], func=AF.Exp, scale=-0.5), reads=["TMP"], writes=["TMP"])
                    S.add("dve", lambda e, c=c: e.tensor_tensor(out=KKN[:, c, :], in0=KKt[:, c, :], in1=TMP[:, c, :], op=ALU.mult), reads=["KKt", "TMP"], writes=["KKN"])
                    S.add("dve", lambda e, c=c, cc_=col(C_KA): e.tensor_scalar(out=KP[:, c, :], in0=AA[:, c, :], scalar1=-1.0, scalar2=cc_, op0=ALU.add, op1=ALU.mult), reads=["AA", "PCOL"], writes=["KP"])
                    S.add("dve", lambda e, c=c: e.scalar_tensor_tensor(out=KP[:, c, :], in0=KP[:, c, :], scalar=1.0, in1=Kr(c), op0=ALU.add, op1=ALU.mult), reads=["KP", "RKV"], writes=["KP"])
                    S.add("dve", lambda e, c=c: e.scalar_tensor_tensor(out=AR[:, c, 0:128], in0=KKN[:, c, :], scalar=-1.0, in1=EPV[:, c, :], op0=ALU.mult, op1=ALU.mult), reads=["KKN", "EPV"], writes=["AR"])
                    S.add("dve", lambda e, c=c: e.tensor_tensor(out=AR[:, c, 128:256], in0=R(c), in1=EP[:, c, :], op=ALU.mult), reads=["RKV", "EP"], writes=["AR"])
                    S.add("dve", lambda e, c=c: e.tensor_tensor(out=TKA[:, c, :], in0=KKN[:, c, :], in1=AA[:, c, :], op=ALU.mult), reads=["KKN", "AA"], writes=["TKA"])
                    S.add("dve", lambda e, c=c: e.tensor_tensor(out=BTIL[:, c, :], in0=TKA[:, c, :], in1=EI[:, c, :], op=ALU.mult), reads=["TKA", "EI"], writes=["BTIL"])
                    S.add("dve", lambda e, c=c: e.tensor_tensor(out=BENDF[:, c, :], in0=TKA[:, c, :], in1=EE[:, c, :], op=ALU.mult), reads=["TKA", "EE"], writes=["BENDF"])
                    S.add("dve", lambda e, c=c: e.tensor_tensor(out=KTIL[:, c, :], in0=KP[:, c, :], in1=EI[:, c, :], op=ALU.mult), reads=["KP", "EI"], writes=["KTIL"])
                    S.add("dve", lambda e, c=c: e.tensor_tensor(out=KENDF[:, c, :], in0=KP[:, c, :], in1=EE[:, c, :], op=ALU.mult), reads=["KP", "EE"], writes=["KENDF"])
                    S.add("dve", lambda e, c=c, cc_=col(C_RK): e.scalar_tensor_tensor(out=PRD[:, c, :], in0=R(c), scalar=cc_, in1=KP[:, c, :], op0=ALU.mult, op1=ALU.mult), reads=["RKV", "PCOL", "KP"], writes=["TKA"])
                    pb, pbn = nb()
                    S.add("pe", lambda e, c=c, pb=pb: e.matmul(pb[:, 0:128], lhsT=cst(K_BD64, K_BD64 + 128), rhs=PRD[:, c, :], start=True, stop=True), reads=["CST", "TKA"], writes=[pbn])
                    S.add("dve", lambda e, c=c, pb=pb, spf=spf: e.tensor_tensor(out=spf[:, 256 + c * 128:256 + (c + 1) * 128], in0=pb[:, 0:128], in1=Vr(c), op=ALU.mult), reads=["RKV"], writes=[pbn, spfn])
                    pbt = es.enter_context(nc.psum_tensor("pbt_%d_%d_%d" % (l, t, c), [128, 1024], BF16)) if False else None
                pb, pbn = nb()
                pbb = pb[:, :].bitcast(BF16)
                for c in range(2):
                    S.add("pe", lambda e, c=c, pbb=pbb: e.transpose(pbb[:, c * 128:(c + 1) * 128], in_=BENDF[:, c, :], identity=IDb[:]), reads=["BENDF", "IDb"], writes=[pbn])
                    S.add("pe", lambda e, c=c, pbb=pbb: e.transpose(pbb[:, (2 + c) * 128:(3 + c) * 128], in_=KENDF[:, c, :], identity=IDb[:]), reads=["KENDF", "IDb"], writes=[pbn])
                S.add("act", lambda e, pbb=pbb: e.activation(out=BENDT[:, :, :], in_=pbb[:, 0:256].rearrange("p (c n) -> p c n", c=2), func=AF.Copy), writes=[pbn, "BENDT"])
                S.add("act", lambda e, pbb=pbb: e.activation(out=KENDT[:, :, :], in_=pbb[:, 256:512].rearrange("p (c n) -> p c n", c=2), func=AF.Copy), writes=[pbn, "KENDT"])
                for h in range(4):
                    c, hs = h // 2, 64 * (h % 2)
                    pq0, pqn0 = PQ[h][0], "PQ%d_0" % h
                    pb, pbn = nb()
                    S.add("pe", lambda e, pb=pb, c=c, hs=hs: e.matmul(pb[:, 0:256], lhsT=BTIL[hs:hs + 64, c, :], rhs=AR[hs:hs + 64, c, :], start=True, stop=True), reads=["BTIL", "AR"], writes=[pbn])
                    S.add("pe", lambda e, pb=pb, c=c, hs=hs: e.matmul(pb[:, 256:384], lhsT=AR[hs:hs + 64, c, 0:128], rhs=BTIL[hs:hs + 64, c, :], start=True, stop=True), reads=["BTIL", "AR"], writes=[pbn])
                    S.add("dve", lambda e, pb=pb, pq0=pq0: e.tensor_tensor(out=pq0[:, 0:128], in0=pb[:, 0:128], in1=cst(K_MU2, K_MU2 + 128), op=ALU.mult), reads=["CST"], writes=[pbn, pqn0])
                    S.add("dve", lambda e, pb=pb, h=h: e.tensor_tensor(out=AKB[h][:, 128:256], in0=pb[:, 128:256], in1=cst(K_MU2 + 128, K_MU2 + 256), op=ALU.mult), reads=["CST"], writes=[pbn, "AKB%d" % h])
                    S.add("dve", lambda e, pb=pb, pq0=pq0: e.tensor_tensor(out=pq0[:, 128:256], in0=pb[:, 256:384], in1=cst(K_MLS, K_MLS + 128), op=ALU.mult), reads=["CST"], writes=[pbn, pqn0])
                    pb, pbn = nb()
                    S.add("pe", lambda e, pb=pb, c=c, hs=hs: e.matmul(pb[:, 0:256], lhsT=KTIL[hs:hs + 64, c, :], rhs=AR[hs:hs + 64, c, :], start=True, stop=True), reads=["KTIL", "AR"], writes=[pbn])
                    S.add("dve", lambda e, pb=pb, h=h: e.tensor_tensor(out=AKB[h][:, 0:128], in0=pb[:, 0:128], in1=cst(K_MU2, K_MU2 + 128), op=ALU.mult), reads=["CST"], writes=[pbn, "AKB%d" % h])
                    S.add("dve", lambda e, pb=pb, h=h: e.tensor_tensor(out=AKB[h][:, 256:384], in0=pb[:, 128:256], in1=cst(K_MU2 + 128, K_MU2 + 256), op=ALU.mult), reads=["CST"], writes=[pbn, "AKB%d" % h])
                    S.add("dve", lambda e, h=h, pq0=pq0: e.tensor_tensor(out=TTf[h][:], in0=pq0[:, 0:128], in1=IDf, op=ALU.add), reads=[pqn0, "CST"], writes=["TTf%d" % h])
                    S.add("pool", lambda e, h=h: e.tensor_copy(out=TTb[h][:], in_=TTf[h][:]), reads=["TTf%d" % h], writes=["TTb%d" % h])
                for lev in range(1, 7):
                    for h in range(4):
                        src_, srcn = PQ[h][(lev - 1) % 2], "PQ%d_%d" % (h, (lev - 1) % 2)
                        dst_, dstn = PQ[h][lev % 2], "PQ%d_%d" % (h, lev % 2)
                        pb, pbn = nb()
                        S.add("pe", lambda e, pb=pb, src_=src_: e.matmul(pb[:, 128:256], lhsT=src_[:, 0:128], rhs=src_[:, 128:256], start=True, stop=True), reads=[srcn], writes=[pbn])
                        if lev < 6:
                            S.add("pe", lambda e, pb=pb, src_=src_: e.matmul(pb[:, 0:128], lhsT=src_[:, 128:256], rhs=src_[:, 0:128], start=True, stop=True), reads=[srcn], writes=[pbn])
                            S.add("act", lambda e, pb=pb, dst_=dst_: e.activation(out=dst_[:, :], in_=pb[:, 0:256], func=AF.Copy), writes=[pbn, dstn])
                        else:
                            S.add("act", lambda e, pb=pb, dst_=dst_: e.activation(out=dst_[:, 128:256], in_=pb[:, 128:256], func=AF.Copy), writes=[pbn, dstn])
                        pb2, pbn2 = nb()
                        S.add("pe", lambda e, pb2=pb2, dst_=dst_, h=h: e.matmul(pb2[:, 0:128], lhsT=dst_[:, 128:256], rhs=TTb[h][:], start=True, stop=True), reads=[dstn, "TTb%d" % h], writes=[pbn2])
                        S.add("dve", lambda e, pb2=pb2, h=h: e.tensor_tensor(out=TTf[h][:], in0=pb2[:, 0:128], in1=TTf[h][:], op=ALU.add), reads=["TTf%d" % h], writes=[pbn2, "TTf%d" % h])
                        S.add("pool", lambda e, h=h: e.tensor_copy(out=TTb[h][:], in_=TTf[h][:]), reads=["TTf%d" % h], writes=["TTb%d" % h])
                for h in range(4):
                    c, hs, hh = h // 2, 64 * (h % 2), h % 2
                    vh = RVb[:, h * 64:(h + 1) * 64]
                    pb, pbn = nb()
                    S.add("pe", lambda e, pb=pb, c=c, hs=hs: e.matmul(pb[:, 0:128], lhsT=AR[hs:hs + 64, c, 0:128], rhs=XSb[hs:hs + 64, c, :], start=True, stop=False), reads=["AR", "XSb"], writes=[pbn])
                    S.add("pe", lambda e, pb=pb, h=h, vh=vh: e.matmul(pb[:, 0:64], lhsT=AKB[h][:, 0:128], rhs=vh, start=False, stop=True), reads=["AKB%d" % h, "RVb"], writes=[pbn])
                    S.add("act", lambda e, pb=pb, h=h: e.activation(out=Wb[h][:], in_=pb[:, 0:128], func=AF.Copy), writes=[pbn, "Wb%d" % h])
                    pb, pbn = nb()
                    S.add("pe", lambda e, pb=pb, h=h: e.matmul(pb[:, 0:128], lhsT=TTb[h][:], rhs=Wb[h][:], start=True, stop=True), reads=["TTb%d" % h, "Wb%d" % h], writes=[pbn])
                    S.add("act", lambda e, pb=pb, h=h: e.activation(out=Ub[h][:], in_=pb[:, 0:128], func=AF.Copy), writes=[pbn, "Ub%d" % h])
                for c in range(2):
                    pby, pbyn = nb()
                    pbx, pbxn = nb()
                    for hh in range(2):
                        h = 2 * c + hh
                        hs = 64 * hh
                        vh = RVb[:, h * 64:(h + 1) * 64]
                        S.add("pe", lambda e, pby=pby, c=c, hs=hs: e.matmul(pby[hs:hs + 64, 0:128], lhsT=XSb[hs:hs + 64, c, 0:64], rhs=AR[hs:hs + 64, c, 128:256], start=True, stop=False), reads=["XSb", "AR"], writes=[pbyn])
                        S.add("pe", lambda e, pby=pby, h=h, hs=hs: e.matmul(pby[hs:hs + 64, 0:128], lhsT=Ub[h][:, 0:64], rhs=AKB[h][:, 128:256], start=False, stop=False), reads=["Ub%d" % h, "AKB%d" % h], writes=[pbyn])
                        S.add("pe", lambda e, pby=pby, h=h, hs=hs, vh=vh: e.matmul(pby[hs:hs + 64, 0:128], lhsT=vh, rhs=AKB[h][:, 256:384], start=False, stop=True), reads=["RVb", "AKB%d" % h], writes=[pbyn])
                        S.add("pe", lambda e, pby=pby, c=c, hs=hs: e.matmul(pby[hs:hs + 64, 128:256], lhsT=XSb[hs:hs + 64, c, 64:128], rhs=AR[hs:hs + 64, c, 128:256], start=True, stop=False), reads=["XSb", "AR"], writes=[pbyn])
                        S.add("pe", lambda e, pby=pby, h=h, hs=hs: e.matmul(pby[hs:hs + 64, 128:256], lhsT=Ub[h][:, 64:128], rhs=AKB[h][:, 128:256], start=False, stop=True), reads=["Ub%d" % h, "AKB%d" % h], writes=[pbyn])
                        S.add("pe", lambda e, pbx=pbx, h=h, hs=hs, c=c: e.matmul(pbx[hs:hs + 64, 0:128], lhsT=BENDT[:, c, hs:hs + 64], rhs=Ub[h][:], start=True, stop=False), reads=["BENDT", "Ub%d" % h], writes=[pbxn])
                        S.add("pe", lambda e, pbx=pbx, h=h, hs=hs, c=c, vh=vh: e.matmul(pbx[hs:hs + 64, 0:64], lhsT=KENDT[:, c, hs:hs + 64], rhs=vh, start=False, stop=True), reads=["KENDT", "RVb"], writes=[pbxn])
                    S.add("dve", lambda e, pby=pby, c=c, spf=spf: e.tensor_copy(out=spf[:, c * 128:(c + 1) * 128], in_=pby[:, 0:128]), writes=[pbyn, spfn])
                    S.add("act", lambda e, pby=pby, c=c, spb=spb: e.activation(out=spb[:, 1024 + c * 128:1024 + (c + 1) * 128], in_=pby[:, 128:256], func=AF.Copy), writes=[pbyn, spbn])
                    S.add("dve", lambda e, pbx=pbx, c=c: e.scalar_tensor_tensor(out=XS[:, c, :], in0=XS[:, c, :], scalar=RSC[:, 2 + c:3 + c], in1=pbx[:, 0:128], op0=ALU.mult, op1=ALU.add), reads=["XS", "RSC"], writes=[pbxn, "XS"])
                    S.add("pool", lambda e, c=c: e.tensor_copy(out=XSb[:, c, :], in_=XS[:, c, :]), reads=["XS"], writes=["XSb"])

                st_rwkv = S.end_defer()
                bank_grp[0] = None
                S.merge([st_attn, st_rwkv])
                S.add("dve", lambda e: e.tensor_tensor_scan(out=CSG[:], data0=ONESf, data1=LG[:], initial=0.0, op0=ALU.mult, op1=ALU.add), reads=["CST", "LG"], writes=["CSG"])
                S.add("act", lambda e: e.activation(out=GEX[:, 0, :], in_=CSG[:], func=AF.Exp, scale=-1.0 / 16), reads=["CSG"], writes=["GEX"])
                S.add("act", lambda e: e.activation(out=GEX[:, 1, :], in_=CSG[:], func=AF.Exp, scale=1.0 / 16), reads=["CSG"], writes=["GEX"])
                S.add("dve", lambda e: e.tensor_scalar(out=GSC[:, 0:1], in0=CSG[:, 127:128], scalar1=-1.0 / 16, scalar2=None, op0=ALU.mult), reads=["CSG"], writes=["GSC"])
                S.add("act", lambda e: e.activation(out=GEX[:, 2, :], in_=CSG[:], func=AF.Exp, scale=1.0 / 16, bias=GSC[:, 0:1]), reads=["CSG", "GSC"], writes=["GEX"])
                S.add("act", lambda e: e.activation(out=GSC[:, 1:2], in_=GSC[:, 0:1], func=AF.Exp), reads=["GSC"], writes=["GSC"])
                S.add("dve", lambda e: e.scalar_tensor_tensor(out=QE[:], in0=GQK[:, 0:128], scalar=float(32 ** -0.5), in1=GEX[:, 0, :], op0=ALU.mult, op1=ALU.mult), reads=["GQK", "GEX"], writes=["QE"])
                S.add("dve", lambda e, spb=spb: e.tensor_scalar(out=spb[:, 1280:1408], in0=QE[:], scalar1=GSC[:, 2:3], scalar2=None, op0=ALU.mult), reads=["QE", "GSC"], writes=[spbn])
                S.add("dve", lambda e: e.tensor_tensor(out=GSC[:, 2:3], in0=GSC[:, 2:3], in1=GSC[:, 1:2], op=ALU.mult), reads=["GSC"], writes=["GSC"])
                S.add("dve", lambda e: e.tensor_tensor(out=KE[:], in0=GQK[:, 128:256], in1=GEX[:, 1, :], op=ALU.mult), reads=["GQK", "GEX"], writes=["KE"])
                S.add("dve", lambda e: e.tensor_tensor(out=KEF[:], in0=GQK[:, 128:256], in1=GEX[:, 2, :], op=ALU.mult), reads=["GQK", "GEX"], writes=["KEF"])
                for h in range(4):
                    S.add("pool", lambda e, h=h: e.tensor_scalar(out=KEP[:, h, :], in0=KE[:], scalar1=CST[:, K_HM + h:K_HM + h + 1], scalar2=None, op0=ALU.mult), reads=["KE", "CST"], writes=["KEP"])
                pb, pbn = nb()
                pbb = pb[:, :].bitcast(BF16)
                S.add("pe", lambda e, pbb=pbb: e.transpose(pbb[:, 0:128], in_=KEF[:], identity=IDb[:]), reads=["KEF", "IDb"], writes=[pbn])
                S.add("act", lambda e, pbb=pbb: e.activation(out=KETM[:], in_=pbb[:, 0:128], func=AF.Copy), writes=[pbn, "KETM"])
                pb, pbn = nb()
                for h in range(4):
                    S.add("pe", lambda e, pb=pb, h=h: e.matmul(pb[:, h * 128:(h + 1) * 128], lhsT=KEP[:, h, :], rhs=QE[:], start=True, stop=True), reads=["KEP", "QE"], writes=[pbn])
                S.add("dve", lambda e, pb=pb: e.tensor_tensor(out=PTG[:], in0=pb[:, :], in1=cst(K_MI4, K_MI4 + 512), op=ALU.mult), reads=["CST"], writes=[pbn, "PTG"])
                for c in range(2):
                    pb, pbn = nb()
                    for hh in range(2):
                        h = 2 * c + hh
                        hs = 64 * hh
                        S.add("pe", lambda e, pb=pb, h=h, hs=hs: e.matmul(pb[hs:hs + 64, 0:128], lhsT=GVb[:, h * 64:(h + 1) * 64], rhs=PTG[:, h * 128:(h + 1) * 128], start=True, stop=False), reads=["GVb", "PTG"], writes=[pbn])
                        S.add("pe", lambda e, pb=pb, h=h, hs=hs: e.matmul(pb[hs:hs + 64, 0:128], lhsT=GSb[:, h * 64:(h + 1) * 64], rhs=QE[:], start=False, stop=True), reads=["GSb", "QE"], writes=[pbn])
                    S.add("dve", lambda e, pb=pb, c=c, spf=spf: e.tensor_copy(out=spf[:, 512 + c * 128:512 + (c + 1) * 128], in_=pb[:, 0:128]), writes=[pbn, spfn])
                pb, pbn = nb()
                S.add("pe", lambda e, pb=pb: e.matmul(pb[:, 0:256], lhsT=KETM[:], rhs=GVb[:], start=True, stop=True), reads=["KETM", "GVb"], writes=[pbn])
                S.add("dve", lambda e, pb=pb: e.tensor_tensor(out=GTMP[:], in0=pb[:, 0:256], in1=cst(K_BDM, K_BDM + 256), op=ALU.mult), reads=["CST"], writes=[pbn, "GTMP"])
                S.add("dve", lambda e: e.scalar_tensor_tensor(out=GS[:], in0=GS[:], scalar=GSC[:, 1:2], in1=GTMP[:], op0=ALU.mult, op1=ALU.add), reads=["GS", "GSC", "GTMP"], writes=["GS"])
                S.add("pool", lambda e: e.tensor_copy(out=GSb[:], in_=GS[:]), reads=["GS"], writes=["GSb"])
                S.add("sp", lambda e, spb=spb, t=t: e.dma_start(out=spb_d[t], in_=spb[:]), reads=[spbn], writes=[("spb", t)], dma_key=("spo", p))
                S.add("sp", lambda e, spf=spf, t=t: e.dma_start(out=spf_d[t], in_=spf[:]), reads=[spfn], writes=[("spf", t)], dma_key=("spo2", p))

            S.barrier()
            for c in range(2):
                S.add("dve", lambda e, c=c: e.tensor_copy(out=XN[:, c * 128:(c + 1) * 128], in_=XS[:, c, :]), reads=["XS"], writes=["XN"])
            S.add("dve", lambda e: e.tensor_copy(out=XN[:, 256:512], in_=GS[:]), reads=["GS"], writes=["XN"])
            S.add("dve", lambda e: e.tensor_copy(out=XN[:, 512:513], in_=GSC[:, 2:3]), reads=["GSC"], writes=["XN"])
            S.add("pool", lambda e: e.memset(XN[:, 513:NCC], 0.0), writes=["XN"])
            S.add("sp", lambda e: e.dma_start(out=ccin_t.ap()[:, 0:NCC], in_=XN[:, 0:NCC]), reads=["XN"], writes=["cc_in"], dma_key="ccd")
            S.add("pool", lambda e: e.collective_compute("AllGather", ALU.bypass, replica_groups=[list(range(NCORES))], ins=[ccin_t.ap().opt()], outs=[ccout_t.ap().opt()]),
                  reads=["cc_in"], writes=["cc_out"], dma_key="cc", inc=1)
            S.add("pool", lambda e: e.memset(SST[:], 0.0), writes=["SST"])
            S.add("pool", lambda e: e.memset(GST[:], 0.0), writes=["GST"])
            for j in range(NCORES - 1):
                S.add("sp", lambda e, j=j: e.dma_start(out=XN[:, 0:NCC], in_=ccout_t.ap()[j * 128:(j + 1) * 128, 0:NCC]), reads=["cc_out"], writes=["XN"], dma_key="ccl")
                fj = FLG[:, j:j + 1]
                for c in range(2):
                    for hh in range(2):
                        S.add("dve", lambda e, c=c, hh=hh: e.tensor_tensor(out=DG[:, hh * 64:(hh + 1) * 64], in0=XN[:, c * 128 + 64:c * 128 + 128], in1=cst(K_BD64 + hh * 64, K_BD64 + (hh + 1) * 64), op=ALU.mult), reads=["XN", "CST"], writes=["DG"])
                    pb, pbn = nb()
                    S.add("pe", lambda e, pb=pb: e.transpose(pb[:, 0:128], in_=DG[:], identity=IDf), reads=["DG", "CST"], writes=[pbn])
                    S.add("act", lambda e, pb=pb: e.activation(out=TMP[:, 0, :], in_=pb[:, 0:128], func=AF.Copy), writes=[pbn, "TMP"])
                    pb, pbn = nb()
                    S.add("pe", lambda e, pb=pb, c=c: e.matmul(pb[:, 0:64], lhsT=TMP[:, 0, :], rhs=SST[:, c, :], start=True, stop=True), reads=["TMP", "SST"], writes=[pbn])
                    S.add("dve", lambda e, pb=pb, c=c: e.tensor_tensor(out=GTMP[:, 0:64], in0=pb[:, 0:64], in1=XN[:, c * 128:c * 128 + 64], op=ALU.add), reads=["XN"], writes=[pbn, "GTMP"])
                    S.add("dve", lambda e, c=c: e.tensor_tensor(out=GTMP[:, 0:64], in0=GTMP[:, 0:64], in1=SST[:, c, :], op=ALU.subtract), reads=["GTMP", "SST"], writes=["GTMP"])
                    S.add("dve", lambda e, c=c, fj=fj: e.scalar_tensor_tensor(out=SST[:, c, :], in0=GTMP[:, 0:64], scalar=fj, in1=SST[:, c, :], op0=ALU.mult, op1=ALU.add), reads=["GTMP", "FLG", "SST"], writes=["SST"])
                S.add("dve", lambda e: e.scalar_tensor_tensor(out=GTMP[:], in0=GST[:], scalar=XN[:, 512:513], in1=XN[:, 256:512], op0=ALU.mult, op1=ALU.add), reads=["GST", "XN"], writes=["GTMP"])
                S.add("dve", lambda e: e.tensor_tensor(out=GTMP[:], in0=GTMP[:], in1=GST[:], op=ALU.subtract), reads=["GTMP", "GST"], writes=["GTMP"])
                S.add("dve", lambda e, fj=fj: e.scalar_tensor_tensor(out=GST[:], in0=GTMP[:], scalar=fj, in1=GST[:], op0=ALU.mult, op1=ALU.add), reads=["GTMP", "FLG", "GST"], writes=["GST"])
            S.add("pool", lambda e: e.tensor_copy(out=SSTb[:], in_=SST[:]), reads=["SST"], writes=["SSTb"])
            S.add("pool", lambda e: e.tensor_copy(out=GSTb[:], in_=GST[:]), reads=["GST"], writes=["GSTb"])

            order = list(range(NT))

            def load_p2(i, src=src):
                t = order[i]
                p = i % 2
                S.add("sp", lambda e: e.dma_start(out=SPB[p][:], in_=spb_d[t]), reads=[("spb", t)], writes=["SPB%d" % p], dma_key=("spi", p))
                S.add("sp", lambda e: e.dma_start(out=SPF[p][:], in_=spf_d[t]), reads=[("spf", t)], writes=["SPF%d" % p], dma_key=("spi2", p))
                S.add("sp", lambda e: e.dma_start(out=XT[p][:], in_=src[t * 128:(t + 1) * 128, :]), writes=["XT%d" % p], dma_key=("xl", p))

            load_p2(0)
            for i in range(NT):
                t = order[i]
                p = i % 2
                if i + 1 < NT:
                    load_p2(i + 1)
                spb, spbn = SPB[p], "SPB%d" % p
                spf, spfn = SPF[p], "SPF%d" % p
                xt, xtn = XT[p], "XT%d" % p
                for c in range(2):
                    pb, pbn = nb()
                    for hh in range(2):
                        hs = 64 * hh
                        S.add("pe", lambda e, pb=pb, c=c, hs=hs, spb=spb: e.matmul(pb[hs:hs + 64, 0:128], lhsT=SSTb[hs:hs + 64, c, :], rhs=spb[hs:hs + 64, 1024 + c * 128:1024 + (c + 1) * 128], start=True, stop=True), reads=["SSTb", spbn], writes=[pbn])
                    S.add("dve", lambda e, pb=pb, c=c, spf=spf: e.tensor_tensor(out=YY[:, c, :], in0=pb[:, 0:128], in1=spf[:, c * 128:(c + 1) * 128], op=ALU.add), reads=[spfn], writes=[pbn, "ELL"])
                    pb, pbn = nb()
                    S.add("pe", lambda e, pb=pb, c=c: e.matmul(pb[:, 0:128], lhsT=BD64S[:], rhs=YY[:, c, :], start=True, stop=True), reads=["BD64S", "ELL"], writes=[pbn])
                    S.add("dve", lambda e, pb=pb, c=c: e.tensor_tensor(out=DD[:, c, :], in0=YY[:, c, :], in1=pb[:, 0:128], op=ALU.subtract), reads=["ELL"], writes=[pbn, "CS"])
                    S.add("act", lambda e, c=c: e.activation(out=KK2[:, c, :], in_=DD[:, c, :], func=AF.Square), reads=["CS"], writes=["KK2"])
                    pb, pbn = nb()
                    S.add("pe", lambda e, pb=pb, c=c: e.matmul(pb[:, 0:128], lhsT=BD64S[:], rhs=KK2[:, c, :], start=True, stop=True), reads=["BD64S", "KK2"], writes=[pbn])
                    S.add("act", lambda e, pb=pb, c=c: e.activation(out=RST[:, c, :], in_=pb[:, 0:128], func=AF.Ln, bias=64e-5), writes=[pbn, "AA"])
                    S.add("act", lambda e, c=c: e.activation(out=RST[:, c, :], in_=RST[:, c, :], func=AF.Exp, scale=-0.5), reads=["AA"], writes=["AA"])
                    S.add("dve", lambda e, c=c: e.tensor_tensor(out=DD[:, c, :], in0=DD[:, c, :], in1=RST[:, c, :], op=ALU.mult), reads=["CS", "AA"], writes=["CS"])
                    S.add("dve", lambda e, c=c: e.tensor_scalar(out=DD[:, c, :], in0=DD[:, c, :], scalar1=PCOL[:, C_LNW + c:C_LNW + c + 1], scalar2=PCOL[:, C_LNB + c:C_LNB + c + 1], op0=ALU.mult, op1=ALU.add), reads=["CS", "PCOL"], writes=["CS"])
                    S.add("dve", lambda e, c=c, spf=spf: e.tensor_tensor(out=DD[:, c, :], in0=DD[:, c, :], in1=spf[:, 256 + c * 128:256 + (c + 1) * 128], op=ALU.add), reads=["CS", spfn], writes=["CS"])
                    S.add("dve", lambda e, c=c, spb=spb: e.tensor_tensor(out=CAT[:, c, :], in0=DD[:, c, :], in1=spb[:, 512 + c * 128:512 + (c + 1) * 128], op=ALU.mult), reads=["CS", spbn], writes=["CAT"])
                for c in range(2):
                    pb, pbn = nb()
                    for hh in range(2):
                        h = 2 * c + hh
                        hs = 64 * hh
                        S.add("pe", lambda e, pb=pb, h=h, hs=hs, spb=spb: e.matmul(pb[hs:hs + 64, 0:128], lhsT=GSTb[:, h * 64:(h + 1) * 64], rhs=spb[:, 1280:1408], start=True, stop=True), reads=["GSTb", spbn], writes=[pbn])
                    S.add("dve", lambda e, pb=pb, c=c, spf=spf: e.tensor_tensor(out=YY[:, c, :], in0=pb[:, 0:128], in1=spf[:, 512 + c * 128:512 + (c + 1) * 128], op=ALU.add), reads=[spfn], writes=[pbn, "ELL"])
                    S.add("act", lambda e, c=c: e.activation(out=KK2[:, c, :], in_=YY[:, c, :], func=AF.Square), reads=["ELL"], writes=["KK2"])
                    pb, pbn = nb()
                    S.add("pe", lambda e, pb=pb, c=c: e.matmul(pb[:, 0:128], lhsT=BD64S[:], rhs=KK2[:, c, :], start=True, stop=True), reads=["BD64S", "KK2"], writes=[pbn])
                    S.add("act", lambda e, pb=pb, c=c: e.activation(out=RST[:, c, :], in_=pb[:, 0:128], func=AF.Ln, bias=1e-5), writes=[pbn, "AA"])
                    S.add("act", lambda e, c=c: e.activation(out=RST[:, c, :], in_=RST[:, c, :], func=AF.Exp, scale=-0.5), reads=["AA"], writes=["AA"])
                    S.add("dve", lambda e, c=c: e.scalar_tensor_tensor(out=YY[:, c, :], in0=YY[:, c, :], scalar=PCOL[:, C_GNW + c:C_GNW + c + 1], in1=RST[:, c, :], op0=ALU.mult, op1=ALU.mult), reads=["ELL", "PCOL", "AA"], writes=["ELL"])
                    S.add("dve", lambda e, c=c, spb=spb: e.tensor_tensor(out=CAT[:, 2 + c, :], in0=YY[:, c, :], in1=spb[:, 512 + (2 + c) * 128:512 + (3 + c) * 128], op=ALU.mult), reads=["ELL", spbn], writes=["CAT"])
                xo, xon = XO[p], "XO%d" % p
                pbs = []
                for half in range(2):
                    pb, pbn = nb()
                    pbs.append((pb, pbn))
                    for k in range(8):
                        lh = spb[:, k * 128:(k + 1) * 128] if k < 4 else CAT[:, k - 4, :]
                        S.add("pe", lambda e, pb=pb, k=k, half=half, lh=lh: e.matmul(pb[:, :], lhsT=lh, rhs=WOUT[:, k, half * 512:(half + 1) * 512], start=(k == 0), stop=(k == 7)), reads=["CAT", spbn, "WOUT"], writes=[pbn])
                    S.add("act", lambda e, pb=pb, half=half: e.activation(out=XN[:, half * 512:(half + 1) * 512], in_=pb[:, :], func=AF.Square, accum_out=SC[:, 4 + half:5 + half]), reads=[], writes=[pbn, "XN", "SC"])
                S.add("dve", lambda e: e.tensor_tensor(out=SC[:, 6:7], in0=SC[:, 4:5], in1=SC[:, 5:6], op=ALU.add), reads=["SC"], writes=["SC"])
                S.add("act", lambda e: e.activation(out=SC[:, 7:8], in_=SC[:, 6:7], func=AF.Ln, scale=1.0 / D, bias=1e-6), reads=["SC"], writes=["SC"])
                S.add("act", lambda e: e.activation(out=SC[:, 8:9], in_=SC[:, 7:8], func=AF.Exp, scale=-0.5), reads=["SC"], writes=["SC"])
                for half in range(2):
                    pb, pbn = pbs[half]
                    S.add("dve", lambda e, pb=pb, half=half, xo=xo: e.scalar_tensor_tensor(out=xo[:, half * 512:(half + 1) * 512], in0=pb[:, :], scalar=SC[:, 8:9], in1=GG[:, half * 512:(half + 1) * 512], op0=ALU.mult, op1=ALU.mult),
                          reads=["SC", "GG"], writes=[pbn, xon])
                S.add("pool", lambda e, xo=xo, xt=xt: e.tensor_tensor(out=xo[:], in0=xo[:], in1=xt[:], op=ALU.add), reads=[xon, xtn], writes=[xon])
                S.add("sp", lambda e, xo=xo, t=t, dst=dst: e.dma_start(out=dst[t * 128:(t + 1) * 128, :], in_=xo[:]), reads=[xon], dma_key=("xs", p))
                if i == NT - 1 and l < L - 1:
                    S.add("sp", lambda e, xo=xo: e.dma_start(out=ccin_t.ap()[:, NCC:NCC + D], in_=xo[:]), reads=[xon], writes=["cc_in"], dma_key="ccd")
                    S.add("pool", lambda e: e.collective_compute("AllGather", ALU.bypass, replica_groups=[list(range(NCORES))], ins=[ccin_t.ap().opt()], outs=[ccout_t.ap().opt()]),
                          reads=["cc_in"], writes=["cc_out"], dma_key="cc", inc=1)

        S.barrier()
        with ExitStack() as es2:
            sems = {e: es2.enter_context(nc.semaphore("s_" + e)) for e in ENGINES}
            dsems = {k: es2.enter_context(nc.semaphore("d_%d" % i)) for i, k in enumerate(S.dma_tot)}
            block = es2.enter_context(nc.Block())
            S.emit(block, sems, dsems)
    return nc, dbg_out


def _consts(core):
    c = np.zeros((128, K_END), np.float32)
    p = np.arange(128)[:, None]
    f = np.arange(128)[None, :]
    c[:, K_ID:K_ID + 128] = np.eye(128, dtype=np.float32)
    c[:, K_MU2:K_MU2 + 128] = (p < f)
    c[:, K_MU2 + 128:K_MU2 + 256] = (p <= f)
    c[:, K_MLS:K_MLS + 128] = (p > f)
    c[:, K_BD64:K_BD64 + 128] = ((p // 64) == (f // 64))
    f2 = np.arange(256)[None, :]
    c[:, K_BDM:K_BDM + 256] = ((p // 32) == (f2 // 64))
    c[:, K_MI4:K_MI4 + 512] = np.tile((p <= f).astype(np.float32), (1, 4))
    for h in range(4):
        c[:, K_HM + h] = (np.arange(128) // 32 == h)
    qi = np.arange(128)[:, None]
    kj = np.arange(256)[None, :]
    dist = qi - kj + 128
    valid = (dist >= 0) & (dist < 128)
    dm = np.where(valid, dist.astype(np.float32), 1.0e9).astype(np.float32)
    c[:, K_AB:K_AB + 256] = dm
    dm0 = dm.copy()
    if core == 0:
        dm0[:, 0:128] = 1.0e9
    c[:, K_AB0:K_AB0 + 256] = dm0
    c[:, K_I2:K_I2 + 64] = (np.arange(128)[:, None] % 64 == np.arange(64)[None, :])
    c[:, K_ONES:K_ONES + 128] = 1.0
    return c


def _pcol(inp, L):
    pc = np.zeros((L, 128, NCOL), np.float32)
    colv = lambda v, n: np.asarray(v, np.float32).reshape(n, 128).T
    for l in range(L):
        pc[l, :, C_GPRE:C_GPRE + 8] = colv(inp["norm_pre"][l], 8)
        pc[l, :, C_GPOST:C_GPOST + 8] = colv(inp["norm_post"][l], 8)
        pc[l, :, C_MUW:C_MUW + 8] = colv(inp["rwkv_mu_w"][l], 8)
        pc[l, :, C_MUA:C_MUA + 8] = colv(inp["rwkv_mu_a"][l], 8)
        pc[l, :, C_ADAB:C_ADAB + 24] = colv(inp["ada_b"][l], 24)
        pc[l, :, C_W0:C_W0 + 2] = colv(inp["rwkv_w0"][l], 2)
        pc[l, :, C_A0:C_A0 + 2] = colv(inp["rwkv_a0"][l], 2)
        pc[l, :, C_KK:C_KK + 2] = colv(inp["rwkv_k_k"][l], 2)
        pc[l, :, C_KA:C_KA + 2] = colv(inp["rwkv_k_a"][l], 2)
        pc[l, :, C_RK:C_RK + 2] = colv(np.asarray(inp["rwkv_r_k"][l]).reshape(256), 2)
        pc[l, :, C_LNW:C_LNW + 2] = colv(inp["rwkv_ln_w"][l], 2)
        pc[l, :, C_LNB:C_LNB + 2] = colv(inp["rwkv_ln_b"][l], 2)
        pc[l, :, C_GKB] = np.asarray(inp["gla_gk_b"][l], np.float32)
        nw = np.tile(np.asarray(inp["gla_norm_w"][l], np.float32), 2)
        pc[l, :, C_GNW] = nw
        pc[l, :, C_GNW + 1] = nw
        pc[l, :, C_SINK:C_SINK + 8] = np.asarray(inp["attn_sinks"][l], np.float32)[None, :]
        pc[l, :, C_C:C_C + 8] = colv(np.asarray(inp["c"]).reshape(D), 8)
    return pc


_CACHE = {}


def _flags(core, ncores):
    f = np.zeros((128, 24), np.float32)
    for j in range(ncores):
        f[:, j] = 1.0 if j < core else 0.0
        f[:, 8 + j] = 1.0 if j == core - 1 else 0.0
    f[:, 16] = 1.0 if core > 0 else 0.0
    return f


def run(inputs, T, L, debug=None, ncores=8):
    inp = {k: np.asarray(v) for k, v in inputs.items()}
    Tc = T // ncores
    key = (Tc, L, ncores, tuple(sorted(debug)) if debug else None)
    if key not in _CACHE:
        _CACHE[key] = build_program(Tc, L, debug, NCORES=ncores)
    nc, dbg = _CACHE[key]
    f32 = lambda a: np.ascontiguousarray(np.asarray(a, np.float32))
    xfull = f32(inp["x"].reshape(-1, D)[:T])
    shared = {
        "w_in": f32(inp["w_in"][:L]), "w_out": f32(inp["w_out"][:L]), "ada_w": f32(inp["ada_w"][:L]),
        "rwkv_w1": f32(inp["rwkv_w1"][:L]), "rwkv_a1": f32(inp["rwkv_a1"][:L]), "rwkv_w2": f32(inp["rwkv_w2"][:L]), "rwkv_a2": f32(inp["rwkv_a2"][:L]),
        "gla_gk1": f32(inp["gla_gk1"][:L]), "gla_gk2": f32(inp["gla_gk2"][:L]), "rwkv_mu_rkv": f32(inp["rwkv_mu_rkv"][:L]),
        "pcol": _pcol(inp, L),
    }
    maps = []
    for c in range(ncores):
        m = dict(shared)
        m["x"] = np.ascontiguousarray(xfull[c * Tc:(c + 1) * Tc])
        m["xh"] = np.ascontiguousarray(xfull[c * Tc - 128:c * Tc]) if c > 0 else np.zeros((128, D), np.float32)
        m["flg"] = _flags(c, ncores)
        m["cst"] = _consts(c)
        maps.append(m)
    res = run_bass_kernel_spmd(nc, maps, core_ids=list(range(ncores)))
    out = np.concatenate([np.asarray(res.results[c]["y"], np.float32) for c in range(ncores)], axis=0).reshape(1, T, D)
    if debug:
        return out, {k: np.asarray(res.results[0]["dbg_" + k]).astype(np.float32) for k in dbg}
    return out


def kernel(**inputs):
    return run(inputs, 16384, 4)
```
